# Optimizing a Trainium2 kernel written in Bass

```python
import jax, jax.numpy as jnp
from jax import lax
import numpy as np

D_MODEL = 1024
BATCH = 16
SEQ = 256
DEPTH = 2
DEC_BATCH = 4
DEC_SEQ = 2048
PAST_LEN = 512

GRID_W = 64
N_HEADS = 8
N_KV_HEADS = 2
HEAD_DIM = 64
GQA_GROUP = N_HEADS // N_KV_HEADS
ATTN_WIDTH = N_HEADS * HEAD_DIM
KV_WIDTH = N_KV_HEADS * HEAD_DIM
WINDOW = 128
BLOCK = 128
ROPE_BASE = 10000.0
CONV_WIDTH = 512
CHUNK = 128
GMLP_WIDTH = 1024
GMLP_GROUPS = 8
GMLP_GROUP_DIM = GMLP_WIDTH // GMLP_GROUPS
D_FF = 2816
EPS = 1e-6
NEG_INF = -1e30

N_EVEN = (DEPTH + 1) // 2
N_ODD = DEPTH // 2
N_ATTN_LAYERS = N_EVEN
EVEN_IN_WIDTH = ATTN_WIDTH + 2 * KV_WIDTH + 3 * CONV_WIDTH

kernel_name = 'hybrid_prefix_diffusion_step'


def rms_norm(x, g):
    xf = x.astype(jnp.float32)
    y = xf * lax.rsqrt(jnp.mean(xf * xf, axis=-1, keepdims=True) + EPS)
    return (y * g.astype(jnp.float32)).astype(x.dtype)


def dwconv3(x, w):
    xp = jnp.pad(x, ((0, 0), (1, 1), (0, 0)))
    return xp[:, :-2] * w[0] + xp[:, 1:-1] * w[1] + xp[:, 2:] * w[2]


def grid_angles(n_tokens):
    rows = n_tokens // GRID_W
    row = jnp.repeat(jnp.arange(rows), GRID_W).astype(jnp.float32)
    col = jnp.tile(jnp.arange(GRID_W), rows).astype(jnp.float32)
    n_freq = HEAD_DIM // 4
    inv = ROPE_BASE ** (-jnp.arange(n_freq, dtype=jnp.float32) / n_freq)
    return row[:, None] * inv, col[:, None] * inv


def rope_half(x, ang):
    n = ang.shape[-1]
    cos = jnp.cos(ang)[None, :, None, :]
    sin = jnp.sin(ang)[None, :, None, :]
    x1 = x[..., :n].astype(jnp.float32)
    x2 = x[..., n:].astype(jnp.float32)
    return jnp.concatenate([x1 * cos - x2 * sin, x2 * cos + x1 * sin], axis=-1).astype(x.dtype)


def axial_rope(x, row_ang, col_ang):
    half = HEAD_DIM // 2
    return jnp.concatenate([rope_half(x[..., :half], row_ang), rope_half(x[..., half:], col_ang)], axis=-1)


def ctx_self_attention(q, k, v, sink):
    B, S = q.shape[0], q.shape[1]
    nb = S // BLOCK
    qb = jnp.moveaxis(q.reshape(B, nb, BLOCK, N_KV_HEADS, GQA_GROUP, HEAD_DIM), 1, 0)
    sink_b = sink.astype(jnp.float32).reshape(1, N_KV_HEADS, GQA_GROUP, 1, 1)
    scale = HEAD_DIM ** -0.5

    def one_block(qi):
        s = jnp.einsum('bqhgd,bkhd->bhgqk', qi, k).astype(jnp.float32) * scale
        s = jnp.concatenate([jnp.broadcast_to(sink_b, s.shape[:-1] + (1,)), s], axis=-1)
        p = jax.nn.softmax(s, axis=-1)[..., 1:].astype(v.dtype)
        return jnp.einsum('bhgqk,bkhd->bqhgd', p, v)

    o = lax.map(one_block, qb)
    return jnp.moveaxis(o, 0, 1).reshape(B, S, ATTN_WIDTH)


def latent_window_attention(q, k, v, kc, vc, sink):
    B, L = q.shape[0], q.shape[1]
    nb = L // BLOCK
    scale = HEAD_DIM ** -0.5
    qb = q.reshape(B, nb, BLOCK, N_KV_HEADS, GQA_GROUP, HEAD_DIM)
    pad = ((0, 0), (BLOCK, BLOCK), (0, 0), (0, 0))
    idx = jnp.arange(nb)[:, None] * BLOCK + jnp.arange(3 * BLOCK)[None, :]
    kw = jnp.pad(k, pad)[:, idx]
    vw = jnp.pad(v, pad)[:, idx]
    qpos = jnp.arange(nb)[:, None, None] * BLOCK + jnp.arange(BLOCK)[None, :, None]
    kpos = idx[:, None, :] - BLOCK
    valid = (jnp.abs(qpos - kpos) <= WINDOW) & (kpos >= 0) & (kpos < L)
    s_loc = jnp.einsum('bnqhgd,bnkhd->bnhgqk', qb, kw).astype(jnp.float32) * scale
    s_loc = jnp.where(valid[None, :, None, None], s_loc, NEG_INF)
    s_ctx = jnp.einsum('bnqhgd,bkhd->bnhgqk', qb, kc).astype(jnp.float32) * scale
    sink_b = jnp.broadcast_to(sink.astype(jnp.float32).reshape(1, 1, N_KV_HEADS, GQA_GROUP, 1, 1),
                              s_loc.shape[:-1] + (1,))
    p = jax.nn.softmax(jnp.concatenate([sink_b, s_loc, s_ctx], axis=-1), axis=-1)
    p_loc = p[..., 1:1 + 3 * BLOCK].astype(v.dtype)
    p_ctx = p[..., 1 + 3 * BLOCK:].astype(v.dtype)
    o = (jnp.einsum('bnhgqk,bnkhd->bnqhgd', p_loc, vw)
         + jnp.einsum('bnhgqk,bkhd->bnqhgd', p_ctx, vc))
    return o.reshape(B, L, ATTN_WIDTH)


def even_projections(h, w_in, q_g, k_g):
    B, T = h.shape[0], h.shape[1]
    z = h @ w_in
    cuts = np.cumsum([ATTN_WIDTH, KV_WIDTH, KV_WIDTH, CONV_WIDTH, CONV_WIDTH]).tolist()
    q, k, v, b_gate, c_gate, hv = jnp.split(z, cuts, axis=-1)
    q = rms_norm(q.reshape(B, T, N_HEADS, HEAD_DIM), q_g)
    k = rms_norm(k.reshape(B, T, N_KV_HEADS, HEAD_DIM), k_g)
    v = v.reshape(B, T, N_KV_HEADS, HEAD_DIM)
    return q, k, v, b_gate, c_gate, hv


def short_conv(b_gate, c_gate, hv, w):
    return b_gate * dwconv3(c_gate * hv, w)


def gmlp_mixer(h, w_in, v_g, w_s, b_s, w_out):
    B, T = h.shape[0], h.shape[1]
    nc = T // CHUNK
    u, v = jnp.split(jax.nn.gelu(h @ w_in), 2, axis=-1)
    v = rms_norm(v, v_g).reshape(B, nc, CHUNK, GMLP_GROUPS, GMLP_GROUP_DIM)
    s = jnp.einsum('gts,bnsgc->bntgc', w_s, v) + b_s.T[:, :, None]
    return (u * s.reshape(B, T, GMLP_WIDTH)) @ w_out


def conv_ffn(h, w_up, conv_w, w_down):
    z = dwconv3(h @ w_up, conv_w)
    g, val = jnp.split(z, 2, axis=-1)
    return (jax.nn.silu(g) * val) @ w_down


def adaln(cond, w, b):
    m = jax.nn.silu(cond) @ w + b
    return [t[:, None, :] for t in jnp.split(m, 6, axis=-1)]


def modulate(x, g, shift, scale):
    return rms_norm(x, g) * (1 + scale) + shift


def setup_inputs(seed: int = 0) -> dict:
    key = jax.random.key(seed)
    ks = jax.random.split(key, 24)

    def nrm(k, shape, scale):
        return jax.random.normal(k, shape, jnp.float32) * scale

    D = D_MODEL
    return {
        'x_prompt': nrm(ks[0], (BATCH, SEQ, D), 1.0),
        'x_sample': nrm(ks[1], (DEC_BATCH, DEC_SEQ, D), 1.0),
        'cache_k': nrm(ks[2], (DEC_BATCH, N_ATTN_LAYERS, PAST_LEN, N_KV_HEADS, HEAD_DIM), 1.0),
        'cache_v': nrm(ks[3], (DEC_BATCH, N_ATTN_LAYERS, PAST_LEN, N_KV_HEADS, HEAD_DIM), 1.0),
        'c': nrm(ks[4], (DEC_BATCH, D), 1.0),
        'c_ctx': nrm(ks[5], (D,), 1.0),
        'ada_w': nrm(ks[6], (DEPTH, D, 6 * D), 0.5 * D ** -0.5),
        'ada_b': nrm(ks[7], (DEPTH, 6 * D), 0.02),
        'norm_mix_g': 1.0 + nrm(ks[8], (DEPTH, D), 0.05),
        'norm_ffn_g': 1.0 + nrm(ks[9], (DEPTH, D), 0.05),
        'w_in_even': nrm(ks[10], (N_EVEN, D, EVEN_IN_WIDTH), D ** -0.5),
        'q_norm_g': 1.0 + nrm(ks[11], (N_EVEN, HEAD_DIM), 0.05),
        'k_norm_g': 1.0 + nrm(ks[12], (N_EVEN, HEAD_DIM), 0.05),
        'sink_logit': nrm(ks[13], (N_EVEN, N_HEADS), 0.5),
        'short_conv_w': nrm(ks[14], (N_EVEN, 3, CONV_WIDTH), 3 ** -0.5),
        'w_out_even': nrm(ks[15], (N_EVEN, ATTN_WIDTH + CONV_WIDTH, D), (ATTN_WIDTH + CONV_WIDTH) ** -0.5),
        'w_in_odd': nrm(ks[16], (N_ODD, D, 2 * GMLP_WIDTH), D ** -0.5),
        'gmlp_norm_g': 1.0 + nrm(ks[17], (N_ODD, GMLP_WIDTH), 0.05),
        'w_spatial': nrm(ks[18], (N_ODD, GMLP_GROUPS, CHUNK, CHUNK), 0.5 * CHUNK ** -0.5),
        'b_spatial': 1.0 + nrm(ks[19], (N_ODD, GMLP_GROUPS, CHUNK), 0.1),
        'w_out_odd': nrm(ks[20], (N_ODD, GMLP_WIDTH, D), GMLP_WIDTH ** -0.5),
        'w_up': nrm(ks[21], (DEPTH, D, 2 * D_FF), D ** -0.5),
        'ffn_conv_w': nrm(ks[22], (DEPTH, 3, 2 * D_FF), 3 ** -0.5),
        'w_down': nrm(ks[23], (DEPTH, D_FF, D), D_FF ** -0.5),
    }


def reference(x_prompt, x_sample, cache_k, cache_v, c, c_ctx, ada_w, ada_b, norm_mix_g, norm_ffn_g,
              w_in_even, q_norm_g, k_norm_g, sink_logit, short_conv_w, w_out_even,
              w_in_odd, gmlp_norm_g, w_spatial, b_spatial, w_out_odd, w_up, ffn_conv_w, w_down):
    row_ang, col_ang = grid_angles(x_sample.shape[1])
    cond_ctx = c_ctx[None, :]
    xp, xs = x_prompt, x_sample
    new_k, new_v = [], []
    for l in range(DEPTH):
        shp, scp, gtp, shp2, scp2, gtp2 = adaln(cond_ctx, ada_w[l], ada_b[l])
        shs, scs, gts, shs2, scs2, gts2 = adaln(c, ada_w[l], ada_b[l])
        hp = modulate(xp, norm_mix_g[l], shp, scp)
        hs = modulate(xs, norm_mix_g[l], shs, scs)
        if l % 2 == 0:
            e = l // 2
            qp, kp, vp, bp, cp, up = even_projections(hp, w_in_even[e], q_norm_g[e], k_norm_g[e])
            mix_p = jnp.concatenate([ctx_self_attention(qp, kp, vp, sink_logit[e]),
                                     short_conv(bp, cp, up, short_conv_w[e])], axis=-1) @ w_out_even[e]
            new_k.append(kp)
            new_v.append(vp)
            qs, ks_, vs, bs, cs, us = even_projections(hs, w_in_even[e], q_norm_g[e], k_norm_g[e])
            qs = axial_rope(qs, row_ang, col_ang)
            ks_ = axial_rope(ks_, row_ang, col_ang)
            attn_s = latent_window_attention(qs, ks_, vs, cache_k[:, e], cache_v[:, e], sink_logit[e])
            mix_s = jnp.concatenate([attn_s, short_conv(bs, cs, us, short_conv_w[e])], axis=-1) @ w_out_even[e]
        else:
            o = l // 2
            mix_p = gmlp_mixer(hp, w_in_odd[o], gmlp_norm_g[o], w_spatial[o], b_spatial[o], w_out_odd[o])
            mix_s = gmlp_mixer(hs, w_in_odd[o], gmlp_norm_g[o], w_spatial[o], b_spatial[o], w_out_odd[o])
        xp = xp + gtp * mix_p
        xs = xs + gts * mix_s
        xp = xp + gtp2 * conv_ffn(modulate(xp, norm_ffn_g[l], shp2, scp2), w_up[l], ffn_conv_w[l], w_down[l])
        xs = xs + gts2 * conv_ffn(modulate(xs, norm_ffn_g[l], shs2, scs2), w_up[l], ffn_conv_w[l], w_down[l])
    new_cache_k = jnp.stack(new_k, axis=1)
    new_cache_v = jnp.stack(new_v, axis=1)
    return (xp, xs, new_cache_k, new_cache_v)
```

```python
import math
from contextlib import ExitStack

import numpy as np
import concourse.bass as bass
import concourse.mybir as mybir
from concourse.bass_utils import run_bass_kernel_spmd

F32 = mybir.dt.float32
BF16 = mybir.dt.bfloat16
AF = mybir.ActivationFunctionType
ALU = mybir.AluOpType

ENGINES = ("pe", "act", "dve", "pool", "sp")


class Acc:
    __slots__ = ("ap", "name", "box")

    def __init__(self, ap, name, box):
        self.ap = ap
        self.name = name
        self.box = box


class T:
    def __init__(self, handle, name, shape, psum=False):
        self.h = handle
        self.name = name
        self.shape = tuple(shape)
        self.psum = psum

    def __getitem__(self, idx):
        if not isinstance(idx, tuple):
            idx = (idx,)
        box = []
        for d, n in enumerate(self.shape):
            if d < len(idx):
                s = idx[d]
                if isinstance(s, slice):
                    lo = 0 if s.start is None else s.start
                    hi = n if s.stop is None else s.stop
                    assert s.step in (None, 1)
                else:
                    lo, hi = s, s + 1
            else:
                lo, hi = 0, n
            assert 0 <= lo < hi <= n, (self.name, idx, self.shape)
            box.append((lo, hi))
        if self.psum:
            return Acc(self.h[idx], self.name, None)
        return Acc(self.h[idx], self.name, tuple(box))

    def whole(self):
        return self[tuple(slice(None) for _ in self.shape)]


def reg(acc_like, ap):
    return Acc(ap, acc_like.name, acc_like.box)


def dram(ap):
    return Acc(ap, "dram:", None)


def _overlap(b1, b2):
    for (l1, h1), (l2, h2) in zip(b1, b2):
        if h1 <= l2 or h2 <= l1:
            return False
    return True


def _covers(b1, b2):
    for (l1, h1), (l2, h2) in zip(b1, b2):
        if l1 > l2 or h1 < h2:
            return False
    return True


class Op:
    __slots__ = ("idx", "eng", "emit", "deps", "is_dma", "sig", "sem", "semval", "waits")

    def __init__(self, idx, eng, emit, is_dma):
        self.idx = idx
        self.eng = eng
        self.emit = emit
        self.deps = set()
        self.is_dma = is_dma
        self.sig = False
        self.sem = None
        self.semval = None
        self.waits = None


class Prog:
    def __init__(self, nc):
        self.nc = nc
        self.ops = []
        self.track = {}
        self.final_dmas = []
        self.fence_deps = set()
        self.fence_last = {}
        self.fence_start = 0

    def fence(self):
        last = dict(self.fence_last)
        dmas = set()
        for op in self.ops[self.fence_start:]:
            if op.is_dma:
                dmas.add(op.idx)
            else:
                last[op.eng] = op.idx
        self.fence_last = last
        self.fence_deps = set(last.values()) | dmas
        self.fence_start = len(self.ops)

    def add(self, eng, emit, reads=(), writes=(), is_dma=False):
        op = Op(len(self.ops), eng, emit, is_dma)
        self.ops.append(op)
        op.deps |= self.fence_deps
        reads = [a for a in reads if not a.name.startswith("dram:")]
        writes = [a for a in writes if not a.name.startswith("dram:")]
        for a in list(reads):
            if a.box is None:
                reads.remove(a)
                writes.append(a)
        writes = [Acc(a.ap, a.name, ((0, 1),)) if a.box is None else a for a in writes]
        for a in reads:
            lst = self.track.setdefault(a.name, [])
            for (box, oi, isw) in lst:
                if isw and _overlap(box, a.box):
                    op.deps.add(oi)
            lst.append((a.box, op.idx, False))
        for a in writes:
            lst = self.track.setdefault(a.name, [])
            keep = []
            for ent in lst:
                box, oi, isw = ent
                if oi == op.idx:
                    keep.append(ent)
                    continue
                if _overlap(box, a.box):
                    op.deps.add(oi)
                    if _covers(a.box, box):
                        continue
                keep.append(ent)
            keep.append((a.box, op.idx, True))
            self.track[a.name] = keep
        op.deps.discard(op.idx)
        return op

    def mm(self, out, lhsT, rhs, start=True, stop=True):
        return self.add("pe", lambda e: e.matmul(out.ap, lhsT.ap, rhs.ap, start=start, stop=stop),
                        reads=[lhsT, rhs], writes=[out])

    def transpose(self, out, in_, ident):
        return self.add("pe", lambda e: e.transpose(out.ap, in_.ap, ident.ap),
                        reads=[in_, ident], writes=[out])

    def act(self, out, in_, func, bias=None, scale=None, accum_out=None):
        reads = [in_]
        kw = {}
        if bias is not None:
            if isinstance(bias, Acc):
                reads.append(bias)
                kw["bias"] = bias.ap
            else:
                kw["bias"] = bias
        if scale is not None:
            if isinstance(scale, Acc):
                reads.append(scale)
                kw["scale"] = scale.ap
            else:
                kw["scale"] = scale
        writes = [out]
        if accum_out is not None:
            writes.append(accum_out)
            kw["accum_out"] = accum_out.ap
        return self.add("act", lambda e: e.activation(out.ap, in_.ap, func, **kw), reads=reads, writes=writes)

    def tt(self, eng, out, in0, in1, op):
        return self.add(eng, lambda e: e.tensor_tensor(out.ap, in0.ap, in1.ap, op), reads=[in0, in1], writes=[out])

    def ts(self, eng, out, in0, s1, s2, op0, op1=None):
        reads = [in0]
        a1, a2 = s1, s2
        if isinstance(s1, Acc):
            reads.append(s1)
            a1 = s1.ap
        if isinstance(s2, Acc):
            reads.append(s2)
            a2 = s2.ap
        if op1 is None:
            return self.add(eng, lambda e: e.tensor_scalar(out.ap, in0.ap, a1, a2, op0), reads=reads, writes=[out])
        return self.add(eng, lambda e: e.tensor_scalar(out.ap, in0.ap, a1, a2, op0, op1), reads=reads, writes=[out])

    def stt(self, eng, out, in0, scalar, in1, op0, op1):
        reads = [in0, in1]
        sc = scalar
        if isinstance(scalar, Acc):
            reads.append(scalar)
            sc = scalar.ap
        return self.add(eng, lambda e: e.scalar_tensor_tensor(out.ap, in0.ap, sc, in1.ap, op0, op1),
                        reads=reads, writes=[out])

    def copy(self, eng, out, in_):
        if eng == "act":
            return self.add(eng, lambda e: e.copy(out.ap, in_.ap), reads=[in_], writes=[out])
        return self.add(eng, lambda e: e.tensor_copy(out.ap, in_.ap), reads=[in_], writes=[out])

    def recip(self, out, in_):
        return self.add("dve", lambda e: e.reciprocal(out.ap, in_.ap), reads=[in_], writes=[out])

    def memset(self, eng, out, val):
        return self.add(eng, lambda e: e.memset(out.ap, val), writes=[out])

    def dma(self, q, out, in_, final=False):
        op = self.add(q, lambda e: e.dma_start(out.ap, in_.ap), reads=[in_], writes=[out], is_dma=True)
        if final:
            self.final_dmas.append(op)
        return op

    def lower(self, sems):
        ops = self.ops
        per_eng = {e: [] for e in ENGINES}
        for op in ops:
            per_eng[op.eng].append(op)
        waited = {e: {} for e in ENGINES}
        waited_dma = {e: set() for e in ENGINES}
        for op in ops:
            need_c = {}
            need_d = []
            for d in op.deps:
                p = ops[d]
                if p.is_dma:
                    if d not in waited_dma[op.eng]:
                        need_d.append(d)
                else:
                    if p.eng == "pe" and op.eng == "pe" and not op.is_dma:
                        continue
                    if need_c.get(p.eng, -1) < d:
                        need_c[p.eng] = d
            w = []
            for pe_, d in need_c.items():
                if waited[op.eng].get(pe_, -1) >= d:
                    continue
                waited[op.eng][pe_] = d
                w.append(d)
            for d in sorted(need_d):
                waited_dma[op.eng].add(d)
                w.append(d)
            op.waits = w
            for d in w:
                ops[d].sig = True
        for op in ops:
            if op.is_dma:
                op.sig = True
        cnt = {e: 0 for e in ENGINES}
        dma_rr = {e: 0 for e in ENGINES}
        dma_cnt = {}
        dma_last = {}
        pre_wait = {}
        for op in ops:
            if not op.sig:
                continue
            if op.is_dma:
                pool = sems["dma_" + op.eng]
                k = dma_rr[op.eng] % len(pool)
                dma_rr[op.eng] += 1
                key = (op.eng, k)
                if key in dma_last:
                    pre_wait[op.idx] = dma_last[key]
                dma_cnt[key] = dma_cnt.get(key, 0) + 16
                op.sem = pool[k]
                op.semval = dma_cnt[key]
                dma_last[key] = (op.sem, op.semval)
            else:
                cnt[op.eng] += 1
                op.sem = sems[op.eng]
                op.semval = cnt[op.eng]
        self.pre_wait = pre_wait
        self.per_eng = per_eng
        self.stats = {e: len(per_eng[e]) for e in ENGINES}
        self.stats["sig"] = dict(cnt)

    def emit_engine(self, eng_name, e):
        ops = self.ops
        for op in self.per_eng[eng_name]:
            if op.idx in self.pre_wait:
                s, v = self.pre_wait[op.idx]
                e.wait_ge(s, v)
            for d in op.waits:
                p = ops[d]
                e.wait_ge(p.sem, p.semval)
            ins = op.emit(e)
            if op.sig:
                ins.then_inc(op.sem, 16 if op.is_dma else 1)
        if eng_name == "sp":
            for op in self.final_dmas:
                e.wait_ge(op.sem, op.semval)


def run_prog(nc, prog):
    with nc.Block() as block:
        @block.sync
        def _(e):
            prog.emit_engine("sp", e)

        @block.tensor
        def _(e):
            prog.emit_engine("pe", e)

        @block.scalar
        def _(e):
            prog.emit_engine("act", e)

        @block.vector
        def _(e):
            prog.emit_engine("dve", e)

        @block.gpsimd
        def _(e):
            prog.emit_engine("pool", e)


D = 1024
KC = 8
NS_IN = 1281
NS_QA = 1280
NS_X1 = 1153
NS_F0 = 1152
NS_G1 = 1152
NS_F1 = 1024
NPR = 256
EPS = 1e-6
DFF = 2816
NFC = 22


class Seg:
    def __init__(self, name, xc0, hc0, n, cond, vb0):
        self.name = name
        self.xc0 = xc0
        self.hc0 = hc0
        self.n = n
        self.cond = cond
        self.vb0 = vb0


SEG_S = Seg("s", 0, 1, NS_IN, 0, 0)
SEG_P0 = Seg("p0", NS_IN, NS_IN + 3, NPR, 1, 11)
SEG_P1 = Seg("p1", NS_IN + NPR, NS_IN + 3 + NPR + 2, NPR, 1, 13)
SEGS = [SEG_S, SEG_P0, SEG_P1]
XC = NS_IN + 2 * NPR
HC = NS_IN + 2 + 2 * (NPR + 2)
NVB = 15
SLOT_G = (0, 2, 1, 3)


def split(n, maxw):
    k = -(-n // maxw)
    base = n // k
    rem = n % k
    out = []
    t = 0
    for i in range(k):
        w = base + (1 if i < rem else 0)
        out.append((t, t + w))
        t += w
    return out


def seg_tiles(ranges, maxw):
    out = []
    for seg, n in ranges:
        for (a, b) in split(n, maxw):
            out.append((seg, a, b))
    return out


def build(stop_after=None):
    nc = bass.Bass("TRN2", target_bir_lowering=False)

    def din(name, shape):
        return nc.dram_tensor(name, list(shape), F32, kind="ExternalInput")

    def dout(name, shape):
        return nc.dram_tensor(name, list(shape), F32, kind="ExternalOutput")

    d_xs = din("xs", [NS_IN, D])
    d_xp = din("xp", [2 * NPR, D])
    d_ck = din("ck", [512, 128])
    d_cv = din("cv", [512, 128])
    d_cond = din("cond", [128, KC, 2])
    d_adaw = din("ada_w", [2, D, 6 * D])
    d_adab = din("ada_bT", [128, 2, 48])
    d_gmix = din("gmixT", [128, 2, KC])
    d_gffn = din("gffnT", [128, 2, KC])
    d_win_e = din("w_in_even", [D, 2304])
    d_wout_e = din("w_out_even", [D, D])
    d_win_o = din("w_in_odd", [D, 2048])
    d_wout_o = din("w_out_odd", [D, D])
    d_wup = din("w_up", [2, D, 2 * DFF])
    d_wdn = din("w_down", [2, DFF, D])
    d_qk = din("qkg", [128, 2])
    d_sink = din("sink4", [4, 2, 64])
    d_sind = din("slotind", [4, 512])
    d_scw = din("scw", [128, 4, 3])
    d_fcw = din("fcw", [128, 2, 2 * NFC, 3])
    d_gvn = din("gvn_b", [128, D])
    d_wst = din("w_sT", [128, 8, 128])
    d_bsp = din("bsp", [128, 8, 128])
    d_cos = din("cos", [128, NS_IN])
    d_sin = din("sin", [128, NS_IN])
    d_cst = din("consts", [128, 4, 128])
    d_msk = din("masks", [128, 2, 256])

    d_ys = dout("ys", [NS_F1, D])
    d_yp = dout("yp", [2 * NPR, D])
    d_nk = dout("nk", [2 * NPR, 128])
    d_nv = dout("nv", [2 * NPR, 128])

    P = Prog(nc)
    es = ExitStack()
    with es:
        def sb(name, shape, dt=F32, stack=es):
            return T(stack.enter_context(nc.sbuf_tensor("sb_" + name, list(shape), dt)), "sb_" + name, shape)

        sems = {}
        for e in ("pe", "act", "dve", "pool"):
            sems[e] = es.enter_context(nc.semaphore("s_" + e))
        for e, n in (("sp", 12), ("pool", 8)):
            sems["dma_" + e] = [es.enter_context(nc.semaphore(f"d_{e}{i}")) for i in range(n)]

        from contextlib import contextmanager

        class _Stop(Exception):
            pass

        @contextmanager
        def phase():
            st_ = ExitStack()
            try:
                yield st_
            except _Stop:
                pass
            P.fence()
            st_.close()

        banks = [T(es.enter_context(nc.psum_tensor(f"ps_b{i}", [128, 512], F32)), f"ps_b{i}", [128, 512], psum=True)
                 for i in range(8)]
        bank_rr = [0]

        def bank(lo=0, hi=8):
            i = lo + bank_rr[0] % (hi - lo)
            bank_rr[0] += 1
            return banks[i]

        xT = sb("xT", [128, KC, XC])
        hT = sb("hT", [128, KC, HC], BF16)
        wslab = [sb(f"wslab{i}", [128, KC, 1024], BF16) for i in range(2)]
        ident = sb("ident", [128, 128])
        ones_bf = sb("ones_bf", [128, 128], BF16)
        bones_bf = sb("bones_bf", [128, 128], BF16)
        rmat_bf = sb("rmat_bf", [128, 128], BF16)
        ident_bf = sb("ident_bf", [128, 128], BF16)
        cond = sb("cond", [128, KC, 2])
        scb = sb("scb", [128, KC, 2], BF16)
        adab = sb("adab", [128, 2, 48])
        modv = sb("modv", [128, 2, 48, 2])
        gmix = sb("gmix", [128, 2, KC])
        gffn = sb("gffn", [128, 2, KC])
        gmod = sb("gmod", [128, 2, 2, KC, 2])
        qkg = sb("qkg", [128, 2])
        scw = sb("scw", [128, 4, 3])
        fcw = sb("fcw", [128, 2, 2 * NFC, 3])
        epst = sb("epst", [128, 1])
        rstd = sb("rstd", [128, 512])
        tmpa = [sb(f"tmpa{i}", [128, 512]) for i in range(2)]

        ws_state = {"i": 0, "pref": {}}

        def w_issue(key, src_fn, ncols_list):
            buf = wslab[ws_state["i"] % 2]
            ws_state["i"] += 1
            off = 0
            for (c0, c1) in ncols_list:
                w = c1 - c0
                P.dma("pool", buf[:, :, off:off + w], dram(src_fn(c0, c1)))
                off += w
            ws_state["pref"][key] = buf
            return buf

        def w_get(key, src_fn, cols):
            if key in ws_state["pref"]:
                return ws_state["pref"].pop(key)
            b = w_issue(key, src_fn, cols)
            ws_state["pref"].pop(key)
            return b

        def src_2d(dt_, rows_pat="(k p) c -> p k c"):
            apv = dt_.ap().rearrange(rows_pat, p=128)
            return lambda c0, c1: apv[:, :, c0:c1]

        def src_3d(dt_, l):
            apv = dt_.ap()[l].rearrange("(k p) c -> p k c", p=128)
            return lambda c0, c1: apv[:, :, c0:c1]

        P.memset("dve", hT.whole(), 0.0)
        P.memset("dve", epst.whole(), EPS)
        for (t_, d_) in ((ident, d_cst.ap()[:, 0, :]), (cond, d_cond.ap()), (adab, d_adab.ap()), (gmix, d_gmix.ap()),
                         (gffn, d_gffn.ap()), (qkg, d_qk.ap()), (scw, d_scw.ap()), (fcw, d_fcw.ap())):
            P.dma("sp", t_.whole(), dram(d_))
        P.dma("pool", ones_bf.whole(), dram(d_cst.ap()[:, 1, :]))
        P.dma("pool", bones_bf.whole(), dram(d_cst.ap()[:, 2, :]))
        P.dma("pool", rmat_bf.whole(), dram(d_cst.ap()[:, 3, :]))
        P.dma("pool", ident_bf.whole(), dram(d_cst.ap()[:, 0, :]))
        P.ts("dve", qkg[:, 0:1], qkg[:, 0:1], 0.125, None, ALU.mult)
        P.act(scb.whole(), cond.whole(), AF.Silu)

        def ada_slab(l, m, slab, mb=None):
            if mb is None:
                mb = bank()
            for fc in range(KC):
                col = fc * 2
                for k in range(KC):
                    P.mm(mb[:, col:col + 2], slab[:, k, fc * 128:(fc + 1) * 128], scb[:, k, :],
                         start=(k == 0), stop=(k == KC - 1))
            for j in range(2):
                src = reg(mb[:, :], mb.h[:, 0:16].rearrange("p (c j) -> p c j", j=2)[:, :, j])
                P.tt("dve", modv[:, l, m * KC:(m + 1) * KC, j], src, adab[:, l, m * KC:(m + 1) * KC], ALU.add)

        def ada_gmod(l):
            for which, gt in ((0, gmix), (1, gffn)):
                for j in range(2):
                    sc = modv[:, l, (1 + 3 * which) * KC:(2 + 3 * which) * KC, j]
                    P.stt("dve", gmod[:, l, which, :, j], sc, 1.0, gt[:, l, :], ALU.add, ALU.mult)

        rstd_main = rstd
        tmpa_main = tmpa

        def mod_ap(l, m, k, j):
            return modv[:, l, m * KC + k:m * KC + k + 1, j]

        def norm_mod(l, which, ranges, sq, sq2=None, rstd2=None, tmp2=None):
            tl = seg_tiles(ranges, 384)
            piped = sq2 is not None

            def bufs(i):
                if piped and i % 2 == 1:
                    return sq2, rstd2, tmp2
                return sq, rstd_main, tmpa_main

            def stage_a(i):
                seg, t0, t1 = tl[i]
                sq_, rstd_, _ = bufs(i)
                n = t1 - t0
                xc = seg.xc0 + t0
                P.act(sq_[:, :, 0:n], xT[:, :, xc:xc + n], AF.Square)
                b = bank()
                for k in range(KC):
                    P.mm(b[:, 0:n], ones_bf[:, :], sq_[:, k, 0:n], start=(k == 0), stop=(k == KC - 1))
                P.act(rstd_[:, 0:n], b[:, 0:n], AF.Sqrt, bias=epst[:, 0:1], scale=1.0 / D)
                P.recip(rstd_[:, 0:n], rstd_[:, 0:n])

            def stage_b(i):
                seg, t0, t1 = tl[i]
                _, rstd_, tmpa_ = bufs(i)
                n = t1 - t0
                xc = seg.xc0 + t0
                hc = seg.hc0 + t0
                j = seg.cond
                for k in range(KC):
                    tm = tmpa_[k % 2]
                    P.tt("dve" if k % 2 == 0 else "pool", tm[:, 0:n], xT[:, k, xc:xc + n], rstd_[:, 0:n], ALU.mult)
                    P.act(hT[:, k, hc:hc + n], tm[:, 0:n], AF.Identity,
                          bias=mod_ap(l, 3 * which, k, j), scale=gmod[:, l, which, k:k + 1, j])

            if not piped:
                for i in range(len(tl)):
                    stage_a(i)
                    stage_b(i)
            else:
                stage_a(0)
                for i in range(len(tl)):
                    if i + 1 < len(tl):
                        stage_a(i + 1)
                    stage_b(i)

        def norm_mod_batched(l, which, ranges, sqs, rstd_all, tmps, part="both"):
            tl = seg_tiles(ranges, 384)
            assert len(tl) <= 8
            offs = []
            o_ = 0
            for (seg, t0, t1) in tl:
                offs.append(o_)
                o_ += t1 - t0
            assert o_ <= rstd_all.shape[1]
            for i, (seg, t0, t1) in enumerate(tl if part != "apply" else []):
                n = t1 - t0
                xc = seg.xc0 + t0
                sq_ = sqs[i % 2]
                P.act(sq_[:, :, 0:n], xT[:, :, xc:xc + n], AF.Square)
                for k in range(KC):
                    P.mm(banks[i][:, 0:n], ones_bf[:, :], sq_[:, k, 0:n], start=(k == 0), stop=(k == KC - 1))
            for i, (seg, t0, t1) in enumerate(tl if part != "apply" else []):
                n = t1 - t0
                P.act(rstd_all[:, offs[i]:offs[i] + n], banks[i][:, 0:n], AF.Ln, bias=epst[:, 0:1], scale=1.0 / D)
            if part != "apply":
                P.act(rstd_all[:, 0:o_], rstd_all[:, 0:o_], AF.Exp, scale=-0.5)
            for i, (seg, t0, t1) in enumerate(tl if part != "stats" else []):
                n = t1 - t0
                xc = seg.xc0 + t0
                hc = seg.hc0 + t0
                j = seg.cond
                rs = rstd_all[:, offs[i]:offs[i] + n]
                for k in range(KC):
                    tm = tmps[(k % 2) * 2 + (k // 2) % 2]
                    if k % 2 == 0:
                        P.tt("dve", tm[:, 0:n], xT[:, k, xc:xc + n], rs, ALU.mult)
                        P.act(hT[:, k, hc:hc + n], tm[:, 0:n], AF.Identity,
                              bias=mod_ap(l, 3 * which, k, j), scale=gmod[:, l, which, k:k + 1, j])
                    else:
                        P.tt("pool", tm[:, 0:n], xT[:, k, xc:xc + n], rs, ALU.mult)
                        P.ts("dve", hT[:, k, hc:hc + n], tm[:, 0:n], gmod[:, l, which, k:k + 1, j],
                             mod_ap(l, 3 * which, k, j), ALU.mult, ALU.add)

        def out_proj_residual(slab, srcT, ranges, l, gate_m, src_is_h=False):
            tl = seg_tiles(ranges, 512)
            for o in range(KC):
                for (seg, t0, t1) in tl:
                    n = t1 - t0
                    xc = seg.xc0 + t0
                    b = bank()
                    for k in range(KC):
                        P.mm(b[:, 0:n], slab[:, k, o * 128:(o + 1) * 128], srcT[:, k, xc:xc + n],
                             start=(k == 0), stop=(k == KC - 1))
                    P.stt("dve", xT[:, o, xc:xc + n], b[:, 0:n], mod_ap(l, gate_m, o, seg.cond),
                          xT[:, o, xc:xc + n], ALU.mult, ALU.add)

        ffn_it = [0]

        def ffn(l, ranges_out, wdn_bufs, actT, tmpf):
            tl = seg_tiles(ranges_out, 510)
            sts = [[]]
            acc = 0
            for t in tl:
                w = t[2] - t[1]
                if acc + w > actT.shape[2]:
                    sts.append([])
                    acc = 0
                sts[-1].append(t)
                acc += w
            up_src = src_3d(d_wup, l)
            dn_ap = d_wdn.ap()[l].rearrange("(k p) c -> p k c", p=128)
            for si, st in enumerate(sts):
                if not st:
                    continue
                offs = []
                o_ = 0
                for (seg, t0, t1) in st:
                    offs.append(o_)
                    o_ += t1 - t0
                assert o_ <= actT.shape[2], (o_, actT.shape)
                nslab = 6

                def up_cols(s_):
                    j0_ = 4 * s_
                    j1_ = min(j0_ + 4, NFC)
                    return [(128 * j0_, 128 * j1_), (DFF + 128 * j0_, DFF + 128 * j1_)]

                for s in range(2):
                    P.dma("pool", wdn_bufs[s].whole(), dram(dn_ap[:, :, 256 * s:256 * (s + 1)]))

                def stage_b(itd):
                    tg, tv, sg, n, j, off = itd
                    P.act(sg[:, 0:n], tg[:, 0:n], AF.Silu)
                    P.tt("pool", actT[:, j, off:off + n], sg[:, 0:n], tv[:, 0:n], ALU.mult)

                pending = None
                for s in range(nslab):
                    j0 = 4 * s
                    j1 = min(j0 + 4, NFC)
                    slab = w_get(("wup", l, si, s), up_src, up_cols(s))
                    if s + 1 < nslab:
                        w_issue(("wup", l, si, s + 1), up_src, up_cols(s + 1))
                    elif si + 1 < len(sts):
                        w_issue(("wup", l, si + 1, 0), up_src, up_cols(0))
                    wv = 128 * (j1 - j0)
                    for jj in range(j1 - j0):
                        j = j0 + jj
                        for ti, (seg, t0, t1) in enumerate(st):
                            n = t1 - t0
                            hc = seg.hc0 + t0 - 1
                            bg = bank()
                            bv = bank()
                            for k in range(KC):
                                P.mm(bg[:, 0:n + 2], slab[:, k, jj * 128:(jj + 1) * 128], hT[:, k, hc:hc + n + 2],
                                     start=(k == 0), stop=(k == KC - 1))
                            for k in range(KC):
                                P.mm(bv[:, 0:n + 2], slab[:, k, wv + jj * 128:wv + (jj + 1) * 128], hT[:, k, hc:hc + n + 2],
                                     start=(k == 0), stop=(k == KC - 1))
                            fset = 3 * (ffn_it[0] % 2)
                            ffn_it[0] += 1
                            tg = tmpf[fset + 0]
                            tv = tmpf[fset + 1]
                            sg = tmpf[fset + 2]
                            P.act(tg[:, 0:n], bg[:, 1:n + 1], AF.Identity, scale=fcw[:, l, j, 1:2])
                            P.act(tv[:, 0:n], bv[:, 1:n + 1], AF.Identity, scale=fcw[:, l, NFC + j, 1:2])
                            P.stt("dve", tg[:, 0:n], bg[:, 0:n], fcw[:, l, j, 0:1], tg[:, 0:n], ALU.mult, ALU.add)
                            P.stt("dve", tg[:, 0:n], bg[:, 2:n + 2], fcw[:, l, j, 2:3], tg[:, 0:n], ALU.mult, ALU.add)
                            P.stt("dve", tv[:, 0:n], bv[:, 0:n], fcw[:, l, NFC + j, 0:1], tv[:, 0:n], ALU.mult, ALU.add)
                            P.stt("dve", tv[:, 0:n], bv[:, 2:n + 2], fcw[:, l, NFC + j, 2:3], tv[:, 0:n], ALU.mult, ALU.add)
                            if pending is not None:
                                stage_b(pending)
                            pending = (tg, tv, sg, n, j, offs[ti])
                stage_b(pending)
                for s in range(4):
                    wb = wdn_bufs[s % 2]
                    if s >= 2:
                        P.dma("pool", wb.whole(), dram(dn_ap[:, :, 256 * s:256 * (s + 1)]))
                    for oo in range(2):
                        o = 2 * s + oo
                        for ti, (seg, t0, t1) in enumerate(st):
                            n = t1 - t0
                            xc = seg.xc0 + t0
                            b = bank()
                            for k in range(NFC):
                                P.mm(b[:, 0:n], wb[:, k, oo * 128:(oo + 1) * 128], actT[:, k, offs[ti]:offs[ti] + n],
                                     start=(k == 0), stop=(k == NFC - 1))
                            P.stt("dve", xT[:, o, xc:xc + n], b[:, 0:n], mod_ap(l, 5, o, seg.cond),
                                  xT[:, o, xc:xc + n], ALU.mult, ALU.add)

        with phase() as ph:
            xst = [sb(f"xst{i}", [128, D], stack=ph) for i in range(2)]
            n0_sq = [sb(f"sqb0{i}", [128, KC, 384], BF16, stack=ph) for i in range(2)]
            n0_rstd = sb("rstdall0m", [128, XC], stack=ph)
            n0_tmp = [sb(f"tmpn0m_{i}", [128, 512], stack=ph) for i in range(2)]
            n0_rng = [(SEG_S, NS_IN), (SEG_P0, NPR), (SEG_P1, NPR)]
            xblocks = []
            for seg, dsrc in ((SEG_S, d_xs), (SEG_P0, d_xp), (SEG_P1, d_xp)):
                roff = NPR if seg is SEG_P1 else 0
                for (t0, t1) in split_blocks(seg.n):
                    xblocks.append((seg, dsrc, roff, t0, t1))

            def load_block(bi):
                seg, dsrc, roff, t0, t1 = xblocks[bi]
                nt = t1 - t0
                st = xst[bi % 2]
                P.dma("sp", st[0:nt, :], dram(dsrc.ap()[roff + t0:roff + t1, :]))
                for half in range(2):
                    b = bank()
                    for kk in range(4):
                        k = half * 4 + kk
                        P.transpose(b[:, kk * 128:kk * 128 + nt], st[0:nt, k * 128:(k + 1) * 128], ident[0:nt, 0:nt])
                    src = reg(b[:, :], b.h[:, :].rearrange("p (a c) -> p a c", a=4)[:, :, 0:nt])
                    dst = xT[:, half * 4:half * 4 + 4, seg.xc0 + t0:seg.xc0 + t1]
                    P.copy("act" if half == 0 else "dve", dst, src)

            w_issue(("ada", 0, 0), src_3d(d_adaw, 0), [(0, D)])
            w_issue(("ada", 0, 1), src_3d(d_adaw, 0), [(D, 2 * D)])
            nb_ = len(xblocks)
            done_ = 0
            for m in range(6):
                upto = min(nb_, (nb_ * (m + 1) + 3) // 4)
                while done_ < upto:
                    load_block(done_)
                    done_ += 1
                slab = w_get(("ada", 0, m), src_3d(d_adaw, 0), [(m * D, (m + 1) * D)])
                ada_slab(0, m, slab)
                if m + 2 < 6:
                    w_issue(("ada", 0, m + 2), src_3d(d_adaw, 0), [((m + 2) * D, (m + 3) * D)])
                elif m == 4:
                    w_issue(("win_e", 0), src_2d(d_win_e), [(0, 768)])
                if m == 3:
                    assert done_ == nb_
                    norm_mod_batched(0, 0, n0_rng, n0_sq, n0_rstd, None, part="stats")
            assert done_ == nb_
            ada_gmod(0)
            norm_mod_batched(0, 0, n0_rng, n0_sq, n0_rstd, tmpa_main + n0_tmp, part="apply")


        def l0_body():
          if stop_after != "load":
            with phase() as ph:
                mixT = sb("mixT", [128, KC, XC], BF16, stack=ph)
                KTd = sb("KTd", [128, 2, XC], BF16, stack=ph)
                Vtok = sb("Vtok", [128, NVB, 2, 128], BF16, stack=ph)
                KcTd = sb("KcTd", [128, 2, 512], BF16, stack=ph)
                Vc = sb("Vc", [128, 4, 2, 128], BF16, stack=ph)
                PT = [sb(f"PT{i}", [128, 512], BF16, stack=ph) for i in range(3)]
                masks = sb("masks", [128, 2, 256], BF16, stack=ph)
                cosT = sb("cosT", [128, NS_IN], BF16, stack=ph)
                sinT = sb("sinT", [128, NS_IN], BF16, stack=ph)
                sink4 = sb("sink4", [4, 2, 64], stack=ph)
                zo4 = sb("zo4", [4, 2, 128], BF16, stack=ph)
                sind = sb("sind", [4, 512], BF16, stack=ph)
                knf = sb("knf", [128, 2 * NPR], stack=ph)
                tq = [sb(f"tq{i}", [128, 512], stack=ph) for i in range(4)]
                qnb = sb("qnb", [128, 512], BF16, stack=ph)
                qnb2 = sb("qnb2", [128, 512], BF16, stack=ph)
                qb16 = sb("qb16", [128, 512], BF16, stack=ph)
                qb16b = sb("qb16b", [128, 512], BF16, stack=ph)
                tq2b = sb("tq2b", [128, 512], stack=ph)
                tq3b = sb("tq3b", [128, 512], stack=ph)
                rope_it = [0]
                rstdq = sb("rstdq", [128, 512], stack=ph)
                tq0b = sb("tq0b", [128, 512], stack=ph)
                qset = [0]

                P.dma("pool", cosT.whole(), dram(d_cos.ap()))
                P.dma("pool", sinT.whole(), dram(d_sin.ap()))
                P.dma("sp", sink4.whole(), dram(d_sink.ap()))
                P.dma("pool", sind.whole(), dram(d_sind.ap()))
                P.dma("pool", masks.whole(), dram(d_msk.ap()))
                P.memset("dve", zo4.whole(), 0.0)
                P.act(zo4[:, :, 64:128], sink4.whole(), AF.Exp)
                P.memset("pool", Vtok.whole(), 1.0)
                P.memset("pool", Vc.whole(), 1.0)

                if True:
                    cst = tq[0]
                    cdup = tq[1]
                    for h_ in range(2):
                        P.dma("pool", Vc[:, :, h_, 0:64],
                              dram(d_cv.ap().rearrange("(b p) c -> p b c", p=128)[:, :, h_ * 64:(h_ + 1) * 64]))
                    P.dma("sp", reg(cst[:, :], cst.h[:, :].rearrange("p (b c) -> p b c", b=4)),
                          dram(d_ck.ap().rearrange("(b p) c -> p b c", p=128)))
                    for cb in range(4):
                        for h in range(2):
                            P.copy("dve", cdup[:, h * 128:h * 128 + 64], cst[:, cb * 128 + h * 64:cb * 128 + (h + 1) * 64])
                            P.copy("act", cdup[:, h * 128 + 64:h * 128 + 128], cst[:, cb * 128 + h * 64:cb * 128 + (h + 1) * 64])
                        b = bank()
                        for h in range(2):
                            P.transpose(b[:, h * 128:(h + 1) * 128], cdup[:, h * 128:(h + 1) * 128], ident[:, :])
                        for h in range(2):
                            P.copy("dve" if h == 0 else "act", KcTd[:, h, cb * 128:(cb + 1) * 128], b[:, h * 128:(h + 1) * 128])


                if stop_after == "l0a":
                    raise _Stop()
                src_e = src_2d(d_win_e)
                slabA = w_get(("win_e", 0), src_e, [(0, 768)])
                w_issue(("win_e", 1), src_e, [(768, 1024), (1280, 1536), (1792, 2048)])
                in_tiles = seg_tiles([(SEG_S, NS_IN), (SEG_P0, NPR), (SEG_P1, NPR)], 510)

                def qk_norm_a(bz, n, si_):
                    sq_ = qnb if si_ == 0 else qnb2
                    P.act(sq_[:, 0:n], bz[:, 0:n], AF.Square)
                    b2 = bank()
                    P.mm(b2[:, 0:n], bones_bf[:, :], sq_[:, 0:n])
                    return b2

                def qk_norm_b(bz, b2, n, gcol, dst_f, si_):
                    rs_ = rstd if si_ == 0 else rstdq
                    P.act(rs_[:, 0:n], b2[:, 0:n], AF.Ln, bias=epst[:, 0:1], scale=1.0 / 64)
                    P.act(rs_[:, 0:n], rs_[:, 0:n], AF.Exp, scale=-0.5)
                    P.stt("dve", dst_f[:, 0:n], bz[:, 0:n], qkg[:, gcol:gcol + 1], rs_[:, 0:n], ALU.mult, ALU.mult)

                def rope(src_f, n, t0, dst):
                    qb_, t2_, t3_ = (qb16, tq[2], tq[3]) if rope_it[0] % 2 == 0 else (qb16b, tq2b, tq3b)
                    rope_it[0] += 1
                    P.copy("act", qb_[:, 0:n], src_f[:, 0:n])
                    b3 = bank()
                    P.mm(b3[:, 0:n], rmat_bf[:, :], qb_[:, 0:n])
                    P.tt("pool", t2_[:, 0:n], src_f[:, 0:n], cosT[:, t0:t0 + n], ALU.mult)
                    P.tt("dve", t3_[:, 0:n], b3[:, 0:n], sinT[:, t0:t0 + n], ALU.mult)
                    P.tt("pool", dst, t2_[:, 0:n], t3_[:, 0:n], ALU.add)

                for (seg, t0, t1) in in_tiles:
                    n = t1 - t0
                    xc = seg.xc0 + t0
                    hc = seg.hc0 + t0
                    is_s = seg is SEG_S
                    for grp in ((0, 1), (2, 3), (4,)):
                        zb = {}
                        for c in grp:
                            bz = bank()
                            for k in range(KC):
                                P.mm(bz[:, 0:n], slabA[:, k, c * 128:(c + 1) * 128], hT[:, k, hc:hc + n],
                                     start=(k == 0), stop=(k == KC - 1))
                            zb[c] = bz
                        sb2 = {}
                        for si_, c in enumerate(grp):
                            sb2[c] = qk_norm_a(zb[c], n, si_)
                        fdst = {}
                        for si_, c in enumerate(grp):
                            if c == 4 and not is_s:
                                fdst[c] = tq[1]
                            else:
                                fdst[c] = tq[0] if si_ == 0 else tq0b
                            qk_norm_b(zb[c], sb2[c], n, 0 if c < 4 else 1, fdst[c], si_)
                        for si_, c in enumerate(grp):
                            f0 = fdst[c]
                            if c < 4:
                                if is_s:
                                    rope(f0, n, t0, mixT[:, c, xc:xc + n])
                                else:
                                    P.copy("act", mixT[:, c, xc:xc + n], f0[:, 0:n])
                            else:
                                if is_s:
                                    rope(f0, n, t0, tq[1][:, 0:n])
                                else:
                                    pc = xc - NS_IN
                                    P.copy("act", knf[:, pc:pc + n], tq[1][:, 0:n])
                                kf = tq[1]
                                P.copy("act", KTd[0:64, 0, xc:xc + n], kf[0:64, 0:n])
                                P.copy("dve", KTd[64:128, 0, xc:xc + n], kf[0:64, 0:n])
                                P.copy("act", KTd[0:64, 1, xc:xc + n], kf[64:128, 0:n])
                                P.copy("dve", KTd[64:128, 1, xc:xc + n], kf[64:128, 0:n])
                for seg in SEGS:
                    for bi, (t0, t1) in enumerate(split_blocks(seg.n)):
                        nt = t1 - t0
                        hc = seg.hc0 + t0
                        bz = bank()
                        for k in range(KC):
                            P.mm(bz[0:nt, 0:128], hT[:, k, hc:hc + nt], slabA[:, k, 640:768],
                                 start=(k == 0), stop=(k == KC - 1))
                        P.copy("act", Vtok[0:nt, seg.vb0 + bi, :, 0:64],
                               reg(bz[0:nt, 0:128], bz.h[0:nt, 0:128].rearrange("p (h d) -> p h d", h=2)))
                        if seg is not SEG_S:
                            pb = seg.vb0 - 11 + bi
                            P.copy("dve", tq[3][0:nt, pb * 128:(pb + 1) * 128], bz[0:nt, 0:128])
                P.dma("sp", dram(d_nv.ap().rearrange("(b p) c -> p b c", p=128)),
                      reg(tq[3][:, :], tq[3].h[:, :].rearrange("p (b c) -> p b c", b=4)), final=True)
                if True:
                    kst = tq[2]
                    for pb in range(4):
                        b = bank()
                        P.transpose(b[:, 0:128], knf[:, pb * 128:(pb + 1) * 128], ident[:, :])
                        P.copy("dve", kst[:, pb * 128:(pb + 1) * 128], b[:, 0:128])
                    P.dma("sp", dram(d_nk.ap().rearrange("(b p) c -> p b c", p=128)),
                          reg(kst[:, :], kst.h[:, :].rearrange("p (b c) -> p b c", b=4)), final=True)

                if stop_after == "l0b":
                    raise _Stop()
                for cs in range(2):
                    slab = w_get(("win_e", 1 + cs), src_e, None)
                    if cs == 0:
                        w_issue(("win_e", 2), src_e, [(1024, 1280), (1536, 1792), (2048, 2304)])
                    else:
                        w_issue(("ada", 1, 0), src_3d(d_adaw, 1), [(0, D)])
                    for jj in range(2):
                        j = cs * 2 + jj
                        for (seg, t0, t1) in in_tiles:
                            n = t1 - t0
                            xc = seg.xc0 + t0
                            hc = seg.hc0 + t0 - 1
                            bB = bank()
                            bC = bank()
                            bH = bank()
                            for bi, bb in enumerate((bB, bC, bH)):
                                co = bi * 256 + jj * 128
                                for k in range(KC):
                                    P.mm(bb[:, 0:n + 2], slab[:, k, co:co + 128], hT[:, k, hc:hc + n + 2],
                                         start=(k == 0), stop=(k == KC - 1))
                            P.copy("act", tq[0][:, 0:n + 2], bH[:, 0:n + 2])
                            P.tt("dve", tq[1][:, 0:n + 2], bC[:, 0:n + 2], tq[0][:, 0:n + 2], ALU.mult)
                            P.act(tq[2][:, 0:n], tq[1][:, 1:n + 1], AF.Identity, scale=scw[:, j, 1:2])
                            P.stt("dve", tq[2][:, 0:n], tq[1][:, 0:n], scw[:, j, 0:1], tq[2][:, 0:n], ALU.mult, ALU.add)
                            P.stt("dve", tq[2][:, 0:n], tq[1][:, 2:n + 2], scw[:, j, 2:3], tq[2][:, 0:n], ALU.mult, ALU.add)
                            P.tt("dve", mixT[:, 4 + j, xc:xc + n], bB[:, 1:n + 1], tq[2][:, 0:n], ALU.mult)

                if stop_after == "l0d":
                    raise _Stop()
                units = []
                it = 0
                for seg in SEGS:
                    is_s = seg is SEG_S
                    nq = NS_QA if is_s else NPR
                    nk = NS_IN if is_s else NPR
                    for qb in range(nq // 128):
                        for h in range(2):
                            kbs = []
                            if is_s:
                                for kb, mk in ((qb - 1, 0), (qb, None), (qb + 1, 1)):
                                    if kb < 0 or kb * 128 >= nk:
                                        continue
                                    kbs.append(("loc", kb, min(128, nk - kb * 128), mk))
                                for cb in range(4):
                                    kbs.append(("ctx", cb, 128, None))
                            else:
                                kbs = [("loc", 0, 128, None), ("loc", 1, 128, None)]
                            for ki, (kind, kb, kn, mk) in enumerate(kbs):
                                units.append(dict(seg=seg, q0=seg.xc0 + qb * 128, h=h, ki=ki, nk=len(kbs), kind=kind,
                                                  kb=kb, kn=kn, mk=mk, it=it, idx=len(units)))
                            it += 1

                def emit_S(u):
                    u["Sbe"] = banks[2 * (u["idx"] % 3)]
                    u["Sbo"] = banks[2 * (u["idx"] % 3) + 1]
                    seg, h, kb, kn, q0 = u["seg"], u["h"], u["kb"], u["kn"], u["q0"]
                    mk = u["mk"]
                    for half in range(2):
                        if u["kind"] == "loc":
                            kt = KTd[half * 64:(half + 1) * 64, h, seg.xc0 + kb * 128:seg.xc0 + kb * 128 + kn]
                        else:
                            kt = KcTd[half * 64:(half + 1) * 64, h, kb * 128:kb * 128 + kn]
                        Sb = u["Sbe"] if half == 0 else u["Sbo"]
                        qv = mixT[half * 64:(half + 1) * 64, 2 * h:2 * h + 2, q0:q0 + 128]
                        so = reg(Sb[0:kn, 0:256], Sb.h[0:kn, 0:256].rearrange("p (a c) -> p a c", a=2))
                        P.mm(so, kt, qv, start=True, stop=(mk is None))
                        if mk is not None:
                            P.mm(Sb[0:kn, 0:256], ident_bf[:, 0:kn], masks[:, mk, :], start=False, stop=True)

                def emit_rest(u):
                    seg, h, kb, kn, q0, ki, it_ = u["seg"], u["h"], u["kb"], u["kn"], u["q0"], u["ki"], u["it"]
                    pt = PT[u["idx"] % 3]
                    P.act(pt[0:kn, 0:256], u["Sbe"][0:kn, 0:256], AF.Exp)
                    P.act(pt[0:kn, 256:512], u["Sbo"][0:kn, 0:256], AF.Exp)
                    OD = banks[6 + it_ % 2]
                    if u["kind"] == "loc":
                        vt = Vtok[0:kn, seg.vb0 + kb, h, :]
                    else:
                        vt = Vc[0:kn, kb, h, :]
                    if seg is SEG_S and 2 <= ki <= 5:
                        P.mm(banks[6 + (it_ + 1) % 2][:, :], ones_bf[:, :], hT[:, 0, 0:512])
                    if ki == 0:
                        P.mm(OD[:, :], zo4[0:4, h, :], sind[0:4, :], start=True, stop=False)
                    P.mm(OD[:, :], vt, pt[0:kn, :], start=False, stop=(ki == u["nk"] - 1))
                    if ki == u["nk"] - 1:
                        den = tq[it_ % 2]
                        on = tq[2 + it_ % 2]
                        if seg is SEG_S:
                            P.recip(den[64:128, :], OD[64:128, :])
                        else:
                            P.act(on[64:128, :], OD[64:128, :], AF.Ln)
                            P.act(den[64:128, :], on[64:128, :], AF.Exp, scale=-1.0)
                        P.copy("dve", den[0:64, :], den[64:128, :])
                        for slot, g in enumerate(SLOT_G):
                            half = g % 2
                            ch = 2 * h + g // 2
                            if half == 0:
                                P.tt("dve", mixT[0:64, ch, q0:q0 + 128], OD[0:64, slot * 128:(slot + 1) * 128],
                                     den[0:64, slot * 128:(slot + 1) * 128], ALU.mult)
                        P.tt("dve", on[0:64, 256:512], OD[0:64, 256:512], den[0:64, 256:512], ALU.mult)
                        for slot, g in enumerate(SLOT_G):
                            half = g % 2
                            ch = 2 * h + g // 2
                            if half == 1:
                                P.copy("dve", mixT[64:128, ch, q0:q0 + 128], on[0:64, slot * 128:(slot + 1) * 128])

                LOOK = 2
                ada_every = max(1, len(units) // 7)
                ada_m = 0
                for i in range(len(units) + LOOK):
                    if i < len(units):
                        emit_S(units[i])
                    if i - LOOK >= 0:
                        emit_rest(units[i - LOOK])
                    if i > 0 and i % ada_every == 0 and ada_m < 6:
                        slab = w_get(("ada", 1, ada_m), src_3d(d_adaw, 1), [(ada_m * D, (ada_m + 1) * D)])
                        if ada_m < 5:
                            w_issue(("ada", 1, ada_m + 1), src_3d(d_adaw, 1), [((ada_m + 1) * D, (ada_m + 2) * D)])
                        else:
                            w_issue(("wout_e", 0), src_2d(d_wout_e), [(0, D)])
                        ada_slab(1, ada_m, slab, mb=banks[2 * ((i + 1) % 3)])
                        ada_m += 1
                assert ada_m == 6
                ada_gmod(1)

                if stop_after == "l0e":
                    raise _Stop()
                slab = w_get(("wout_e", 0), None, None)
                w_issue(("wup", 0, 0, 0), src_3d(d_wup, 0), [(0, 512), (DFF, DFF + 512)])
                ktd_flat = KTd.h[:, :, :].rearrange("p a c -> p (a c)")[:, 0:KC * 384].rearrange("p (k c) -> p k c", k=KC)
                vtk_flat = Vtok.h[:, :, :, :].rearrange("p a b c -> p (a b c)")[:, 0:KC * 384].rearrange("p (k c) -> p k c", k=KC)
                sq_views = [(KTd.whole(), ktd_flat), (Vtok.whole(), vtk_flat)]
                rs_bufs = [tq[0], tq[1], tq0b, tq2b, tq3b, rstdq]
                tm_bufs = tmpa_main + [tq[2], tq[3]]
                otl = seg_tiles([(SEG_S, NS_X1), (SEG_P0, NPR), (SEG_P1, NPR)], 384)
                assert len(otl) <= len(rs_bufs)
                def op_tile(ti_):
                    seg, t0, t1 = otl[ti_]
                    n = t1 - t0
                    xc = seg.xc0 + t0
                    j = seg.cond
                    for o in range(KC):
                        b = bank()
                        for k in range(KC):
                            P.mm(b[:, 0:n], slab[:, k, o * 128:(o + 1) * 128], mixT[:, k, xc:xc + n],
                                 start=(k == 0), stop=(k == KC - 1))
                        P.stt("dve", xT[:, o, xc:xc + n], b[:, 0:n], mod_ap(0, 2, o, j),
                              xT[:, o, xc:xc + n], ALU.mult, ALU.add)

                def nrm_tile(ti_):
                    seg, t0, t1 = otl[ti_]
                    n = t1 - t0
                    xc = seg.xc0 + t0
                    hc = seg.hc0 + t0
                    j = seg.cond
                    sqw, sqa = sq_views[ti_ % 2]
                    P.act(reg(sqw, sqa[:, :, 0:n]), xT[:, :, xc:xc + n], AF.Square)
                    b = bank()
                    for k in range(KC):
                        P.mm(b[:, 0:n], ones_bf[:, :], reg(sqw, sqa[:, k, 0:n]), start=(k == 0), stop=(k == KC - 1))
                    rs = rs_bufs[ti_]
                    P.act(rs[:, 0:n], b[:, 0:n], AF.Ln, bias=epst[:, 0:1], scale=1.0 / D)
                    P.act(rs[:, 0:n], rs[:, 0:n], AF.Exp, scale=-0.5)
                    for k in range(KC):
                        tm = tm_bufs[(k % 2) * 2 + (k // 2) % 2]
                        if k % 2 == 0:
                            P.tt("dve", tm[:, 0:n], xT[:, k, xc:xc + n], rs[:, 0:n], ALU.mult)
                            P.act(hT[:, k, hc:hc + n], tm[:, 0:n], AF.Identity,
                                  bias=mod_ap(0, 3, k, j), scale=gmod[:, 0, 1, k:k + 1, j])
                        else:
                            P.tt("pool", tm[:, 0:n], xT[:, k, xc:xc + n], rs[:, 0:n], ALU.mult)
                            P.ts("dve", hT[:, k, hc:hc + n], tm[:, 0:n], gmod[:, 0, 1, k:k + 1, j],
                                 mod_ap(0, 3, k, j), ALU.mult, ALU.add)

                op_tile(0)
                for ti_ in range(len(otl)):
                    if ti_ + 1 < len(otl):
                        op_tile(ti_ + 1)
                    nrm_tile(ti_)

        try:
            l0_body()
        except _Stop:
            pass

        def ffn_phase(l, ranges_norm, ranges_out, next_key_issue):
            if ranges_norm is not None:
              with phase() as ph:
                  sqb = sb(f"sqbf{l}", [128, KC, 384], BF16, stack=ph)
                  sqb2 = sb(f"sqbg{l}", [128, KC, 384], BF16, stack=ph)
                  rstd_all = sb(f"rstdall{l}", [128, XC], stack=ph)
                  tmp2 = [sb(f"tmpn{l}_{i}", [128, 512], stack=ph) for i in range(2)]
                  norm_mod_batched(l, 1, ranges_norm, [sqb, sqb2], rstd_all, tmpa_main + tmp2)
            with phase() as ph:
                actT = sb(f"actT{l}", [128, NFC, 900], BF16, stack=ph)
                wdn_bufs = [sb(f"wdn{l}_{i}", [128, NFC, 256], BF16, stack=ph) for i in range(2)]
                tmpf = [sb(f"tmpf{l}_{i}", [128, 512], stack=ph) for i in range(6)]
                ffn(l, ranges_out, wdn_bufs, actT, tmpf)
                if next_key_issue is not None:
                    next_key_issue()

        EARLY = ("load", "l0a", "l0b", "l0d", "l0e", "l0mix")
        if stop_after not in EARLY:
            ffn_phase(0, None,
                      [(SEG_S, NS_F0), (SEG_P0, NPR), (SEG_P1, NPR)],
                      lambda: w_issue(("win_o", 0), src_2d(d_win_o), [(0, D)]))

        rng1 = [(SEG_S, NS_G1), (SEG_P0, NPR), (SEG_P1, NPR)]
        if stop_after not in EARLY + ("l0",):
            with phase() as phn:
                sqb = sb("sqb1", [128, KC, 384], BF16, stack=phn)
                sqb2 = sb("sqb1b", [128, KC, 384], BF16, stack=phn)
                rstd_all = sb("rstdall1m", [128, XC], stack=phn)
                tmp2 = [sb(f"tmpn1b_{i}", [128, 512], stack=phn) for i in range(2)]
                norm_mod_batched(1, 0, rng1, [sqb, sqb2], rstd_all, tmpa_main + tmp2)
            with phase() as ph:
                uT = sb("uT", [128, KC, XC], BF16, stack=ph)
                gvn = sb("gvn", [128, D], stack=ph)
                bsp = sb("bsp", [128, 8, 128], stack=ph)
                wst = sb("wst", [128, 8, 128], BF16, stack=ph)
                gv_s = [sb(f"gv{i}", [128, D], stack=ph) for i in range(2)]
                junk_s = [sb(f"junk{i}", [128, D], BF16, stack=ph) for i in range(2)]
                vn_s = [sb(f"vn{i}", [128, D], BF16, stack=ph) for i in range(2)]
                ssq_s = [sb(f"ssq{i}", [128, 2], stack=ph) for i in range(2)]
                tg = [sb(f"tg{i}", [128, 512], stack=ph) for i in range(4)]
                gchunk = [0]
                P.dma("sp", gvn.whole(), dram(d_gvn.ap()))
                P.dma("sp", bsp.whole(), dram(d_bsp.ap()))
                P.dma("pool", wst.whole(), dram(d_wst.ap()))
                src_o = src_2d(d_win_o)
                slabU = w_get(("win_o", 0), src_o, [(0, D)])
                slabV = w_issue(("win_o", 1), src_o, [(D, 2 * D)])
                ws_state["pref"].pop(("win_o", 1))
                for o in range(KC):
                    for (seg, t0, t1) in seg_tiles(rng1, 512):
                        n = t1 - t0
                        b = bank()
                        for k in range(KC):
                            P.mm(b[:, 0:n], slabU[:, k, o * 128:(o + 1) * 128], hT[:, k, seg.hc0 + t0:seg.hc0 + t1],
                                 start=(k == 0), stop=(k == KC - 1))
                        P.act(uT[:, o, seg.xc0 + t0:seg.xc0 + t1], b[:, 0:n], AF.Gelu_apprx_tanh)
                def g_stage_a(seg, cb, cp):
                    t0 = cb * 128
                    hc = seg.hc0 + t0
                    gv, junk, vn, ssq = gv_s[cp], junk_s[cp], vn_s[cp], ssq_s[cp]
                    for half in range(2):
                        b = bank()
                        for k in range(KC):
                            P.mm(b[:, :], hT[:, k, hc:hc + 128], slabV[:, k, half * 512:(half + 1) * 512],
                                 start=(k == 0), stop=(k == KC - 1))
                        P.act(gv[:, half * 512:(half + 1) * 512], b[:, :], AF.Gelu_apprx_tanh)
                    P.memset("dve", ssq[:, 0:1], 0.0)
                    P.act(junk[:, :], gv[:, :], AF.Square, accum_out=ssq[:, 0:1])
                    P.act(ssq[:, 1:2], ssq[:, 0:1], AF.Sqrt, bias=epst[:, 0:1], scale=1.0 / D)
                    P.recip(ssq[:, 1:2], ssq[:, 1:2])
                    P.stt("dve", vn[:, :], gv[:, :], ssq[:, 1:2], gvn[:, :], ALU.mult, ALU.mult)

                def g_stage_b(seg, cb, cp):
                    xc = seg.xc0 + cb * 128
                    vn = vn_s[cp]
                    for gh in range(2):
                        b = bank()
                        for gg in range(4):
                            g = gh * 4 + gg
                            P.mm(b[:, gg * 128:(gg + 1) * 128], vn[:, g * 128:(g + 1) * 128], wst[:, g, :])
                        tt_ = tg[2 * cp + gh]
                        bview = reg(bsp[:, gh * 4:gh * 4 + 4, :], bsp.h[:, gh * 4:gh * 4 + 4, :].rearrange("p a c -> p (a c)"))
                        P.tt("dve", tt_[:, :], b[:, :], bview, ALU.add)
                        tview = reg(tt_[:, :], tt_.h[:, :].rearrange("p (a c) -> p a c", a=4))
                        uv = uT[:, gh * 4:gh * 4 + 4, xc:xc + 128]
                        P.tt("pool", uv, uv, tview, ALU.mult)

                w_issue(("wout_o", 0), src_2d(d_wout_o), [(0, D)])
                chunks = [(seg, cb) for seg, ntok in rng1 for cb in range(ntok // 128)]
                for i, (seg, cb) in enumerate(chunks):
                    g_stage_a(seg, cb, i % 2)
                    if i > 0:
                        g_stage_b(chunks[i - 1][0], chunks[i - 1][1], (i - 1) % 2)
                g_stage_b(chunks[-1][0], chunks[-1][1], (len(chunks) - 1) % 2)
                slabO = w_get(("wout_o", 0), src_2d(d_wout_o), [(0, D)])
                w_issue(("wup", 1, 0, 0), src_3d(d_wup, 1), [(0, 512), (DFF, DFF + 512)])
                out_proj_residual(slabO, uT, rng1, 1, 2)

        if stop_after not in EARLY + ("l0", "l1mix"):
            ffn_phase(1, [(SEG_S, NS_F1 + 1), (SEG_P0, NPR), (SEG_P1, NPR)],
                      [(SEG_S, NS_F1), (SEG_P0, NPR), (SEG_P1, NPR)], None)

        with phase() as ph:
            yst = [sb(f"yst{i}", [128, D], stack=ph) for i in range(2)]
            blk = 0
            for seg, ddst, nout in ((SEG_S, d_ys, NS_F1), (SEG_P0, d_yp, NPR), (SEG_P1, d_yp, NPR)):
                roff = NPR if seg is SEG_P1 else 0
                for tb in range(nout // 128):
                    t0 = tb * 128
                    st = yst[blk % 2]
                    blk += 1
                    for half in range(2):
                        b = bank()
                        for kk in range(4):
                            k = half * 4 + kk
                            P.transpose(b[:, kk * 128:(kk + 1) * 128], xT[:, k, seg.xc0 + t0:seg.xc0 + t0 + 128], ident[:, :])
                        P.copy("act" if half == 0 else "dve", st[:, half * 512:(half + 1) * 512], b[:, :])
                    P.dma("sp", dram(ddst.ap()[roff + t0:roff + t0 + 128, :]), st.whole(), final=True)

        P.lower(sems)
        run_prog(nc, P)
    return nc, P


def split_blocks(n):
    out = []
    t = 0
    while t < n:
        out.append((t, min(t + 128, n)))
        t += 128
    return out


_CACHE = {}


def _consts():
    ident = np.eye(128, dtype=np.float32)
    ones = np.ones((128, 128), np.float32)
    bones = np.zeros((128, 128), np.float32)
    bones[:64, :64] = 1.0
    bones[64:, 64:] = 1.0
    rmat = np.zeros((128, 128), np.float32)
    for m in range(128):
        j = m % 32
        if j < 16:
            rmat[m + 16, m] = -1.0
        else:
            rmat[m - 16, m] = 1.0
    cst = np.stack([ident, ones, bones, rmat], axis=1)
    kk = np.arange(128)[:, None]
    qq = np.arange(128)[None, :]
    m_prev = np.where(qq <= kk, 0.0, -30000.0).astype(np.float32)
    m_next = np.where(kk <= qq, 0.0, -30000.0).astype(np.float32)
    masks = np.stack([np.tile(m_prev, (1, 2)), np.tile(m_next, (1, 2))], axis=1)
    return np.ascontiguousarray(cst), np.ascontiguousarray(masks)


def _rope_tables(pos):
    n_freq = 16
    inv = 10000.0 ** (-np.arange(n_freq, dtype=np.float64) / n_freq)
    row = (pos // 64).astype(np.float64)
    col = (pos % 64).astype(np.float64)
    ra = row[None, :] * inv[:, None]
    ca = col[None, :] * inv[:, None]
    ang = np.zeros((64, len(pos)), np.float64)
    for d in range(64):
        f = d % 16
        ang[d] = ra[f] if d < 32 else ca[f]
    ang = np.concatenate([ang, ang], axis=0)
    return np.cos(ang).astype(np.float32), np.sin(ang).astype(np.float32)


def kernel(x_prompt, x_sample, cache_k, cache_v, c, c_ctx, ada_w, ada_b, norm_mix_g, norm_ffn_g,
           w_in_even, q_norm_g, k_norm_g, sink_logit, short_conv_w, w_out_even,
           w_in_odd, gmlp_norm_g, w_spatial, b_spatial, w_out_odd, w_up, ffn_conv_w, w_down, _stop_after=None):
    f = lambda a: np.ascontiguousarray(np.asarray(a, dtype=np.float32))
    x_prompt, x_sample, cache_k, cache_v = f(x_prompt), f(x_sample), f(cache_k), f(cache_v)
    c, c_ctx, ada_w, ada_b = f(c), f(c_ctx), f(ada_w), f(ada_b)
    key = ("nc", _stop_after)
    if key not in _CACHE:
        _CACHE[key] = build(_stop_after)
    nc, _ = _CACHE[key]
    cst, masks = _consts()

    def fm(v, nchunk):
        return np.ascontiguousarray(f(v).reshape(nchunk, 128).T)

    shared = {
        "ada_w": ada_w,
        "ada_bT": np.ascontiguousarray(np.stack([fm(ada_b[l], 48) for l in range(2)], axis=1)),
        "gmixT": np.ascontiguousarray(np.stack([fm(norm_mix_g[l], 8) for l in range(2)], axis=1)),
        "gffnT": np.ascontiguousarray(np.stack([fm(norm_ffn_g[l], 8) for l in range(2)], axis=1)),
        "w_in_even": f(w_in_even)[0], "w_out_even": f(w_out_even)[0],
        "w_in_odd": f(w_in_odd)[0], "w_out_odd": f(w_out_odd)[0],
        "w_up": f(w_up), "w_down": f(w_down),
        "qkg": np.ascontiguousarray(np.stack([np.tile(f(q_norm_g)[0], 2), np.tile(f(k_norm_g)[0], 2)], axis=1)),
        "sink4": np.ascontiguousarray(np.broadcast_to(
            f(sink_logit)[0].reshape(2, 4)[:, list(SLOT_G)].T[:, :, None], (4, 2, 64))),
        "slotind": np.ascontiguousarray(np.repeat(np.eye(4, dtype=np.float32), 128, axis=1)),
        "gvn_b": np.ascontiguousarray(np.broadcast_to(f(gmlp_norm_g)[0][None, :], (128, D))),
        "consts": cst, "masks": masks,
    }
    scw_n = f(short_conv_w)[0]
    fcw_n = f(ffn_conv_w)
    wsp = f(w_spatial)[0]
    bspn = f(b_spatial)[0]

    def variant(mirror):
        tap = [2, 1, 0] if mirror else [0, 1, 2]
        scw = np.ascontiguousarray(scw_n[tap].reshape(3, 4, 128).transpose(2, 1, 0))
        fcw = np.ascontiguousarray(fcw_n[:, tap].reshape(2, 3, 44, 128).transpose(3, 0, 2, 1))
        w = wsp[:, ::-1, ::-1] if mirror else wsp
        w_sT = np.ascontiguousarray(w.transpose(2, 0, 1))
        b = bspn[:, ::-1] if mirror else bspn
        bsp = np.ascontiguousarray(np.broadcast_to(b[None], (128, 8, 128)))
        return {"scw": scw, "fcw": fcw, "w_sT": w_sT, "bsp": bsp}

    var = [variant(False), variant(True)]
    in_maps = []
    for core in range(8):
        b = core // 2
        mirror = core % 2 == 1
        if not mirror:
            pos = np.arange(0, NS_IN)
        else:
            pos = 2047 - np.arange(0, NS_IN)
        xs = np.ascontiguousarray(x_sample[b][pos])
        cos, sin = _rope_tables(pos)
        cond = np.ascontiguousarray(np.stack([c[b].reshape(8, 128).T, c_ctx.reshape(8, 128).T], axis=2))
        m = dict(shared)
        m.update(var[1 if mirror else 0])
        m.update({
            "xs": xs,
            "xp": np.ascontiguousarray((x_prompt[2 * core:2 * core + 2][:, ::-1] if mirror
                                        else x_prompt[2 * core:2 * core + 2]).reshape(2 * NPR, D)),
            "ck": np.ascontiguousarray(cache_k[b, 0].reshape(512, 128)),
            "cv": np.ascontiguousarray(cache_v[b, 0].reshape(512, 128)),
            "cond": cond, "cos": cos, "sin": sin,
        })
        in_maps.append(m)
    res = run_bass_kernel_spmd(nc, in_maps, core_ids=list(range(8)))
    yp = np.zeros((16, NPR, D), np.float32)
    ys = np.zeros((4, 2048, D), np.float32)
    nk = np.zeros((16, 1, NPR, 2, 64), np.float32)
    nv = np.zeros((16, 1, NPR, 2, 64), np.float32)
    for core in range(8):
        r = res.results[core]
        b = core // 2
        sl = slice(None, None, -1) if core % 2 == 1 else slice(None)
        yp[2 * core:2 * core + 2] = r["yp"].reshape(2, NPR, D)[:, sl]
        nk[2 * core:2 * core + 2, 0] = r["nk"].reshape(2, NPR, 2, 64)[:, sl]
        nv[2 * core:2 * core + 2, 0] = r["nv"].reshape(2, NPR, 2, 64)[:, sl]
        if core % 2 == 0:
            ys[b, 0:1024] = r["ys"]
        else:
            ys[b, 1024:2048] = r["ys"][::-1]
    return (yp, ys, nk, nv)
```

```python
import math
from contextlib import ExitStack

import numpy as np
import concourse.bass as bass
import concourse.mybir as mybir
from concourse.bass_utils import run_bass_kernel_spmd

F32 = mybir.dt.float32
BF16 = mybir.dt.bfloat16
AF = mybir.ActivationFunctionType
ALU = mybir.AluOpType

ENGINES = ("pe", "act", "dve", "pool", "sp")


class Acc:
    __slots__ = ("ap", "name", "box")

    def __init__(self, ap, name, box):
        self.ap = ap
        self.name = name
        self.box = box


class T:
    def __init__(self, handle, name, shape, psum=False):
        self.h = handle
        self.name = name
        self.shape = tuple(shape)
        self.psum = psum

    def __getitem__(self, idx):
        if not isinstance(idx, tuple):
            idx = (idx,)
        box = []
        for d, n in enumerate(self.shape):
            if d < len(idx):
                s = idx[d]
                if isinstance(s, slice):
                    lo = 0 if s.start is None else s.start
                    hi = n if s.stop is None else s.stop
                    assert s.step in (None, 1)
                else:
                    lo, hi = s, s + 1
            else:
                lo, hi = 0, n
            assert 0 <= lo < hi <= n, (self.name, idx, self.shape)
            box.append((lo, hi))
        if self.psum:
            return Acc(self.h[idx], self.name, None)
        return Acc(self.h[idx], self.name, tuple(box))

    def whole(self):
        return self[tuple(slice(None) for _ in self.shape)]


def reg(acc_like, ap):
    return Acc(ap, acc_like.name, acc_like.box)


def dram(ap):
    return Acc(ap, "dram:", None)


def _overlap(b1, b2):
    for (l1, h1), (l2, h2) in zip(b1, b2):
        if h1 <= l2 or h2 <= l1:
            return False
    return True


def _covers(b1, b2):
    for (l1, h1), (l2, h2) in zip(b1, b2):
        if l1 > l2 or h1 < h2:
            return False
    return True


class Op:
    __slots__ = ("idx", "eng", "emit", "deps", "is_dma", "sig", "sem", "semval", "waits")

    def __init__(self, idx, eng, emit, is_dma):
        self.idx = idx
        self.eng = eng
        self.emit = emit
        self.deps = set()
        self.is_dma = is_dma
        self.sig = False
        self.sem = None
        self.semval = None
        self.waits = None


class Prog:
    def __init__(self, nc):
        self.nc = nc
        self.ops = []
        self.track = {}
        self.final_dmas = []
        self.fence_deps = set()
        self.fence_last = {}
        self.fence_start = 0

    def fence(self):
        last = dict(self.fence_last)
        dmas = set()
        for op in self.ops[self.fence_start:]:
            if op.is_dma:
                dmas.add(op.idx)
            else:
                last[op.eng] = op.idx
        self.fence_last = last
        self.fence_deps = set(last.values()) | dmas
        self.fence_start = len(self.ops)

    def add(self, eng, emit, reads=(), writes=(), is_dma=False):
        op = Op(len(self.ops), eng, emit, is_dma)
        self.ops.append(op)
        op.deps |= self.fence_deps
        reads = [a for a in reads if not a.name.startswith("dram:")]
        writes = [a for a in writes if not a.name.startswith("dram:")]
        for a in list(reads):
            if a.box is None:
                reads.remove(a)
                writes.append(a)
        writes = [Acc(a.ap, a.name, ((0, 1),)) if a.box is None else a for a in writes]
        for a in reads:
            lst = self.track.setdefault(a.name, [])
            for (box, oi, isw) in lst:
                if isw and _overlap(box, a.box):
                    op.deps.add(oi)
            lst.append((a.box, op.idx, False))
        for a in writes:
            lst = self.track.setdefault(a.name, [])
            keep = []
            for ent in lst:
                box, oi, isw = ent
                if oi == op.idx:
                    keep.append(ent)
                    continue
                if _overlap(box, a.box):
                    op.deps.add(oi)
                    if _covers(a.box, box):
                        continue
                keep.append(ent)
            keep.append((a.box, op.idx, True))
            self.track[a.name] = keep
        op.deps.discard(op.idx)
        return op

    def mm(self, out, lhsT, rhs, start=True, stop=True):
        return self.add("pe", lambda e: e.matmul(out.ap, lhsT.ap, rhs.ap, start=start, stop=stop),
                        reads=[lhsT, rhs], writes=[out])

    def transpose(self, out, in_, ident):
        return self.add("pe", lambda e: e.transpose(out.ap, in_.ap, ident.ap),
                        reads=[in_, ident], writes=[out])

    def act(self, out, in_, func, bias=None, scale=None, accum_out=None):
        reads = [in_]
        kw = {}
        if bias is not None:
            if isinstance(bias, Acc):
                reads.append(bias)
                kw["bias"] = bias.ap
            else:
                kw["bias"] = bias
        if scale is not None:
            if isinstance(scale, Acc):
                reads.append(scale)
                kw["scale"] = scale.ap
            else:
                kw["scale"] = scale
        writes = [out]
        if accum_out is not None:
            writes.append(accum_out)
            kw["accum_out"] = accum_out.ap
        return self.add("act", lambda e: e.activation(out.ap, in_.ap, func, **kw), reads=reads, writes=writes)

    def tt(self, eng, out, in0, in1, op):
        return self.add(eng, lambda e: e.tensor_tensor(out.ap, in0.ap, in1.ap, op), reads=[in0, in1], writes=[out])

    def ts(self, eng, out, in0, s1, s2, op0, op1=None):
        reads = [in0]
        a1, a2 = s1, s2
        if isinstance(s1, Acc):
            reads.append(s1)
            a1 = s1.ap
        if isinstance(s2, Acc):
            reads.append(s2)
            a2 = s2.ap
        if op1 is None:
            return self.add(eng, lambda e: e.tensor_scalar(out.ap, in0.ap, a1, a2, op0), reads=reads, writes=[out])
        return self.add(eng, lambda e: e.tensor_scalar(out.ap, in0.ap, a1, a2, op0, op1), reads=reads, writes=[out])

    def stt(self, eng, out, in0, scalar, in1, op0, op1):
        reads = [in0, in1]
        sc = scalar
        if isinstance(scalar, Acc):
            reads.append(scalar)
            sc = scalar.ap
        return self.add(eng, lambda e: e.scalar_tensor_tensor(out.ap, in0.ap, sc, in1.ap, op0, op1),
                        reads=reads, writes=[out])

    def copy(self, eng, out, in_):
        if eng == "act":
            return self.add(eng, lambda e: e.copy(out.ap, in_.ap), reads=[in_], writes=[out])
        return self.add(eng, lambda e: e.tensor_copy(out.ap, in_.ap), reads=[in_], writes=[out])

    def recip(self, out, in_):
        return self.add("dve", lambda e: e.reciprocal(out.ap, in_.ap), reads=[in_], writes=[out])

    def memset(self, eng, out, val):
        return self.add(eng, lambda e: e.memset(out.ap, val), writes=[out])

    def dma(self, q, out, in_, final=False):
        op = self.add(q, lambda e: e.dma_start(out.ap, in_.ap), reads=[in_], writes=[out], is_dma=True)
        if final:
            self.final_dmas.append(op)
        return op

    def lower(self, sems):
        ops = self.ops
        per_eng = {e: [] for e in ENGINES}
        for op in ops:
            per_eng[op.eng].append(op)
        waited = {e: {} for e in ENGINES}
        waited_dma = {e: set() for e in ENGINES}
        for op in ops:
            need_c = {}
            need_d = []
            for d in op.deps:
                p = ops[d]
                if p.is_dma:
                    if d not in waited_dma[op.eng]:
                        need_d.append(d)
                else:
                    if p.eng == "pe" and op.eng == "pe" and not op.is_dma:
                        continue
                    if need_c.get(p.eng, -1) < d:
                        need_c[p.eng] = d
            w = []
            for pe_, d in need_c.items():
                if waited[op.eng].get(pe_, -1) >= d:
                    continue
                waited[op.eng][pe_] = d
                w.append(d)
            for d in sorted(need_d):
                waited_dma[op.eng].add(d)
                w.append(d)
            op.waits = w
            for d in w:
                ops[d].sig = True
        for op in ops:
            if op.is_dma:
                op.sig = True
        cnt = {e: 0 for e in ENGINES}
        dma_rr = {e: 0 for e in ENGINES}
        dma_cnt = {}
        dma_last = {}
        pre_wait = {}
        for op in ops:
            if not op.sig:
                continue
            if op.is_dma:
                pool = sems["dma_" + op.eng]
                k = dma_rr[op.eng] % len(pool)
                dma_rr[op.eng] += 1
                key = (op.eng, k)
                if key in dma_last:
                    pre_wait[op.idx] = dma_last[key]
                dma_cnt[key] = dma_cnt.get(key, 0) + 16
                op.sem = pool[k]
                op.semval = dma_cnt[key]
                dma_last[key] = (op.sem, op.semval)
            else:
                cnt[op.eng] += 1
                op.sem = sems[op.eng]
                op.semval = cnt[op.eng]
        self.pre_wait = pre_wait
        self.per_eng = per_eng
        self.stats = {e: len(per_eng[e]) for e in ENGINES}
        self.stats["sig"] = dict(cnt)

    def emit_engine(self, eng_name, e):
        ops = self.ops
        for op in self.per_eng[eng_name]:
            if op.idx in self.pre_wait:
                s, v = self.pre_wait[op.idx]
                e.wait_ge(s, v)
            for d in op.waits:
                p = ops[d]
                e.wait_ge(p.sem, p.semval)
            ins = op.emit(e)
            if op.sig:
                ins.then_inc(op.sem, 16 if op.is_dma else 1)
        if eng_name == "sp":
            for op in self.final_dmas:
                e.wait_ge(op.sem, op.semval)


def run_prog(nc, prog):
    with nc.Block() as block:
        @block.sync
        def _(e):
            prog.emit_engine("sp", e)

        @block.tensor
        def _(e):
            prog.emit_engine("pe", e)

        @block.scalar
        def _(e):
            prog.emit_engine("act", e)

        @block.vector
        def _(e):
            prog.emit_engine("dve", e)

        @block.gpsimd
        def _(e):
            prog.emit_engine("pool", e)


D = 1024
KC = 8
NS_IN = 1281
NS_QA = 1280
NS_X1 = 1153
NS_F0 = 1152
NS_G1 = 1152
NS_F1 = 1024
NPR = 256
EPS = 1e-6
DFF = 2816
NFC = 22


class Seg:
    def __init__(self, name, xc0, hc0, n, cond, vb0):
        self.name = name
        self.xc0 = xc0
        self.hc0 = hc0
        self.n = n
        self.cond = cond
        self.vb0 = vb0


SEG_S = Seg("s", 0, 1, NS_IN, 0, 0)
SEG_P0 = Seg("p0", NS_IN, NS_IN + 3, NPR, 1, 11)
SEG_P1 = Seg("p1", NS_IN + NPR, NS_IN + 3 + NPR + 2, NPR, 1, 13)
SEGS = [SEG_S, SEG_P0, SEG_P1]
XC = NS_IN + 2 * NPR
HC = NS_IN + 2 + 2 * (NPR + 2)
NVB = 15
SLOT_G = (0, 2, 1, 3)


def split(n, maxw):
    k = -(-n // maxw)
    base = n // k
    rem = n % k
    out = []
    t = 0
    for i in range(k):
        w = base + (1 if i < rem else 0)
        out.append((t, t + w))
        t += w
    return out


def seg_tiles(ranges, maxw):
    out = []
    for seg, n in ranges:
        for (a, b) in split(n, maxw):
            out.append((seg, a, b))
    return out


def build(stop_after=None):
    nc = bass.Bass("TRN2", target_bir_lowering=False)

    def din(name, shape):
        return nc.dram_tensor(name, list(shape), F32, kind="ExternalInput")

    def dout(name, shape):
        return nc.dram_tensor(name, list(shape), F32, kind="ExternalOutput")

    d_xs = din("xs", [NS_IN, D])
    d_xp = din("xp", [2 * NPR, D])
    d_ck = din("ck", [512, 128])
    d_cv = din("cv", [512, 128])
    d_cond = din("cond", [128, KC, 2])
    d_adaw = din("ada_w", [2, D, 6 * D])
    d_adab = din("ada_bT", [128, 2, 48])
    d_gmix = din("gmixT", [128, 2, KC])
    d_gffn = din("gffnT", [128, 2, KC])
    d_win_e = din("w_in_even", [D, 2304])
    d_wout_e = din("w_out_even", [D, D])
    d_win_o = din("w_in_odd", [D, 2048])
    d_wout_o = din("w_out_odd", [D, D])
    d_wup = din("w_up", [2, D, 2 * DFF])
    d_wdn = din("w_down", [2, DFF, D])
    d_qk = din("qkg", [128, 2])
    d_sink = din("sink4", [4, 2, 64])
    d_sind = din("slotind", [4, 512])
    d_scw = din("scw", [128, 4, 3])
    d_fcw = din("fcw", [128, 2, 2 * NFC, 3])
    d_gvn = din("gvn_b", [128, D])
    d_wst = din("w_sT", [128, 8, 128])
    d_bsp = din("bsp", [128, 8, 128])
    d_cos = din("cos", [128, NS_IN])
    d_sin = din("sin", [128, NS_IN])
    d_cst = din("consts", [128, 4, 128])
    d_msk = din("masks", [128, 2, 512])

    d_ys = dout("ys", [NS_F1, D])
    d_yp = dout("yp", [2 * NPR, D])
    d_nk = dout("nk", [2 * NPR, 128])
    d_nv = dout("nv", [2 * NPR, 128])

    P = Prog(nc)
    es = ExitStack()
    with es:
        def sb(name, shape, dt=F32, stack=es):
            return T(stack.enter_context(nc.sbuf_tensor("sb_" + name, list(shape), dt)), "sb_" + name, shape)

        sems = {}
        for e in ("pe", "act", "dve", "pool"):
            sems[e] = es.enter_context(nc.semaphore("s_" + e))
        for e, n in (("sp", 12), ("pool", 8)):
            sems["dma_" + e] = [es.enter_context(nc.semaphore(f"d_{e}{i}")) for i in range(n)]

        from contextlib import contextmanager

        class _Stop(Exception):
            pass

        @contextmanager
        def phase():
            st_ = ExitStack()
            try:
                yield st_
            except _Stop:
                pass
            P.fence()
            st_.close()

        banks = [T(es.enter_context(nc.psum_tensor(f"ps_b{i}", [128, 512], F32)), f"ps_b{i}", [128, 512], psum=True)
                 for i in range(8)]
        bank_rr = [0]

        def bank(lo=0, hi=8):
            i = lo + bank_rr[0] % (hi - lo)
            bank_rr[0] += 1
            return banks[i]

        xT = sb("xT", [128, KC, XC])
        hT = sb("hT", [128, KC, HC], BF16)
        wslab = [sb(f"wslab{i}", [128, KC, 1024], BF16) for i in range(2)]
        ident = sb("ident", [128, 128])
        ones_bf = sb("ones_bf", [128, 128], BF16)
        bones_bf = sb("bones_bf", [128, 128], BF16)
        rmat_bf = sb("rmat_bf", [128, 128], BF16)
        ident_bf = sb("ident_bf", [128, 128], BF16)
        cond = sb("cond", [128, KC, 2])
        scb = sb("scb", [128, KC, 2], BF16)
        adab = sb("adab", [128, 2, 48])
        modv = sb("modv", [128, 2, 48, 2])
        gmix = sb("gmix", [128, 2, KC])
        gffn = sb("gffn", [128, 2, KC])
        gmod = sb("gmod", [128, 2, 2, KC, 2])
        qkg = sb("qkg", [128, 2])
        scw = sb("scw", [128, 4, 3])
        fcw = sb("fcw", [128, 2, 2 * NFC, 3])
        epst = sb("epst", [128, 1])
        rstd = sb("rstd", [128, 512])
        tmpa = [sb(f"tmpa{i}", [128, 512]) for i in range(2)]

        ws_state = {"i": 0, "pref": {}}

        def w_issue(key, src_fn, ncols_list):
            buf = wslab[ws_state["i"] % 2]
            ws_state["i"] += 1
            off = 0
            for (c0, c1) in ncols_list:
                w = c1 - c0
                P.dma("pool", buf[:, :, off:off + w], dram(src_fn(c0, c1)))
                off += w
            ws_state["pref"][key] = buf
            return buf

        def w_get(key, src_fn, cols):
            if key in ws_state["pref"]:
                return ws_state["pref"].pop(key)
            b = w_issue(key, src_fn, cols)
            ws_state["pref"].pop(key)
            return b

        def src_2d(dt_, rows_pat="(k p) c -> p k c"):
            apv = dt_.ap().rearrange(rows_pat, p=128)
            return lambda c0, c1: apv[:, :, c0:c1]

        def src_3d(dt_, l):
            apv = dt_.ap()[l].rearrange("(k p) c -> p k c", p=128)
            return lambda c0, c1: apv[:, :, c0:c1]

        P.memset("dve", hT.whole(), 0.0)
        P.memset("dve", epst.whole(), EPS)
        for (t_, d_) in ((ident, d_cst.ap()[:, 0, :]), (cond, d_cond.ap()), (adab, d_adab.ap()), (gmix, d_gmix.ap()),
                         (gffn, d_gffn.ap()), (qkg, d_qk.ap()), (scw, d_scw.ap()), (fcw, d_fcw.ap())):
            P.dma("sp", t_.whole(), dram(d_))
        P.dma("pool", ones_bf.whole(), dram(d_cst.ap()[:, 1, :]))
        P.dma("pool", bones_bf.whole(), dram(d_cst.ap()[:, 2, :]))
        P.dma("pool", rmat_bf.whole(), dram(d_cst.ap()[:, 3, :]))
        P.dma("pool", ident_bf.whole(), dram(d_cst.ap()[:, 0, :]))
        P.ts("dve", qkg[:, 0:1], qkg[:, 0:1], 0.125, None, ALU.mult)
        P.act(scb.whole(), cond.whole(), AF.Silu)

        def ada_slab(l, m, slab, mb=None):
            if mb is None:
                mb = bank()
            for fc in range(KC):
                col = fc * 2
                for k in range(KC):
                    P.mm(mb[:, col:col + 2], slab[:, k, fc * 128:(fc + 1) * 128], scb[:, k, :],
                         start=(k == 0), stop=(k == KC - 1))
            for j in range(2):
                src = reg(mb[:, :], mb.h[:, 0:16].rearrange("p (c j) -> p c j", j=2)[:, :, j])
                P.tt("dve", modv[:, l, m * KC:(m + 1) * KC, j], src, adab[:, l, m * KC:(m + 1) * KC], ALU.add)

        def ada_gmod(l):
            for which, gt in ((0, gmix), (1, gffn)):
                for j in range(2):
                    sc = modv[:, l, (1 + 3 * which) * KC:(2 + 3 * which) * KC, j]
                    P.stt("dve", gmod[:, l, which, :, j], sc, 1.0, gt[:, l, :], ALU.add, ALU.mult)

        rstd_main = rstd
        tmpa_main = tmpa

        def mod_ap(l, m, k, j):
            return modv[:, l, m * KC + k:m * KC + k + 1, j]

        def norm_mod(l, which, ranges, sq, sq2=None, rstd2=None, tmp2=None):
            tl = seg_tiles(ranges, 384)
            piped = sq2 is not None

            def bufs(i):
                if piped and i % 2 == 1:
                    return sq2, rstd2, tmp2
                return sq, rstd_main, tmpa_main

            def stage_a(i):
                seg, t0, t1 = tl[i]
                sq_, rstd_, _ = bufs(i)
                n = t1 - t0
                xc = seg.xc0 + t0
                P.act(sq_[:, :, 0:n], xT[:, :, xc:xc + n], AF.Square)
                b = bank()
                for k in range(KC):
                    P.mm(b[:, 0:n], ones_bf[:, :], sq_[:, k, 0:n], start=(k == 0), stop=(k == KC - 1))
                P.act(rstd_[:, 0:n], b[:, 0:n], AF.Sqrt, bias=epst[:, 0:1], scale=1.0 / D)
                P.recip(rstd_[:, 0:n], rstd_[:, 0:n])

            def stage_b(i):
                seg, t0, t1 = tl[i]
                _, rstd_, tmpa_ = bufs(i)
                n = t1 - t0
                xc = seg.xc0 + t0
                hc = seg.hc0 + t0
                j = seg.cond
                for k in range(KC):
                    tm = tmpa_[k % 2]
                    P.tt("dve" if k % 2 == 0 else "pool", tm[:, 0:n], xT[:, k, xc:xc + n], rstd_[:, 0:n], ALU.mult)
                    P.act(hT[:, k, hc:hc + n], tm[:, 0:n], AF.Identity,
                          bias=mod_ap(l, 3 * which, k, j), scale=gmod[:, l, which, k:k + 1, j])

            if not piped:
                for i in range(len(tl)):
                    stage_a(i)
                    stage_b(i)
            else:
                stage_a(0)
                for i in range(len(tl)):
                    if i + 1 < len(tl):
                        stage_a(i + 1)
                    stage_b(i)

        def norm_mod_batched(l, which, ranges, sqs, rstd_all, tmps, part="both"):
            tl = seg_tiles(ranges, 384)
            assert len(tl) <= 8
            offs = []
            o_ = 0
            for (seg, t0, t1) in tl:
                offs.append(o_)
                o_ += t1 - t0
            assert o_ <= rstd_all.shape[1]
            for i, (seg, t0, t1) in enumerate(tl if part != "apply" else []):
                n = t1 - t0
                xc = seg.xc0 + t0
                sq_ = sqs[i % 2]
                P.act(sq_[:, :, 0:n], xT[:, :, xc:xc + n], AF.Square)
                for k in range(KC):
                    P.mm(banks[i][:, 0:n], ones_bf[:, :], sq_[:, k, 0:n], start=(k == 0), stop=(k == KC - 1))
            for i, (seg, t0, t1) in enumerate(tl if part != "apply" else []):
                n = t1 - t0
                P.act(rstd_all[:, offs[i]:offs[i] + n], banks[i][:, 0:n], AF.Ln, bias=epst[:, 0:1], scale=1.0 / D)
            if part != "apply":
                P.act(rstd_all[:, 0:o_], rstd_all[:, 0:o_], AF.Exp, scale=-0.5)
            for i, (seg, t0, t1) in enumerate(tl if part != "stats" else []):
                n = t1 - t0
                xc = seg.xc0 + t0
                hc = seg.hc0 + t0
                j = seg.cond
                rs = rstd_all[:, offs[i]:offs[i] + n]
                for k in range(KC):
                    tm = tmps[(k % 2) * 2 + (k // 2) % 2]
                    if k % 2 == 0:
                        P.tt("dve", tm[:, 0:n], xT[:, k, xc:xc + n], rs, ALU.mult)
                        P.act(hT[:, k, hc:hc + n], tm[:, 0:n], AF.Identity,
                              bias=mod_ap(l, 3 * which, k, j), scale=gmod[:, l, which, k:k + 1, j])
                    else:
                        P.tt("pool", tm[:, 0:n], xT[:, k, xc:xc + n], rs, ALU.mult)
                        P.ts("dve", hT[:, k, hc:hc + n], tm[:, 0:n], gmod[:, l, which, k:k + 1, j],
                             mod_ap(l, 3 * which, k, j), ALU.mult, ALU.add)

        def out_proj_residual(slab, srcT, ranges, l, gate_m, src_is_h=False):
            tl = seg_tiles(ranges, 512)
            for o in range(KC):
                for (seg, t0, t1) in tl:
                    n = t1 - t0
                    xc = seg.xc0 + t0
                    b = bank()
                    for k in range(KC):
                        P.mm(b[:, 0:n], slab[:, k, o * 128:(o + 1) * 128], srcT[:, k, xc:xc + n],
                             start=(k == 0), stop=(k == KC - 1))
                    P.stt("dve", xT[:, o, xc:xc + n], b[:, 0:n], mod_ap(l, gate_m, o, seg.cond),
                          xT[:, o, xc:xc + n], ALU.mult, ALU.add)

        ffn_it = [0]

        def ffn(l, ranges_out, wdn_bufs, actT, tmpf):
            tl = seg_tiles(ranges_out, 510)
            sts = [[]]
            acc = 0
            for t in tl:
                w = t[2] - t[1]
                if acc + w > actT.shape[2]:
                    sts.append([])
                    acc = 0
                sts[-1].append(t)
                acc += w
            up_src = src_3d(d_wup, l)
            dn_ap = d_wdn.ap()[l].rearrange("(k p) c -> p k c", p=128)
            for si, st in enumerate(sts):
                if not st:
                    continue
                offs = []
                o_ = 0
                for (seg, t0, t1) in st:
                    offs.append(o_)
                    o_ += t1 - t0
                assert o_ <= actT.shape[2], (o_, actT.shape)
                nslab = 6

                def up_cols(s_):
                    j0_ = 4 * s_
                    j1_ = min(j0_ + 4, NFC)
                    return [(128 * j0_, 128 * j1_), (DFF + 128 * j0_, DFF + 128 * j1_)]

                for s in range(2):
                    P.dma("pool", wdn_bufs[s].whole(), dram(dn_ap[:, :, 256 * s:256 * (s + 1)]))

                def stage_b(itd):
                    tg, tv, sg, n, j, off = itd
                    P.act(sg[:, 0:n], tg[:, 0:n], AF.Silu)
                    P.tt("pool", actT[:, j, off:off + n], sg[:, 0:n], tv[:, 0:n], ALU.mult)

                pending = None
                for s in range(nslab):
                    j0 = 4 * s
                    j1 = min(j0 + 4, NFC)
                    slab = w_get(("wup", l, si, s), up_src, up_cols(s))
                    if s + 1 < nslab:
                        w_issue(("wup", l, si, s + 1), up_src, up_cols(s + 1))
                    elif si + 1 < len(sts):
                        w_issue(("wup", l, si + 1, 0), up_src, up_cols(0))
                    wv = 128 * (j1 - j0)
                    for jj in range(j1 - j0):
                        j = j0 + jj
                        for ti, (seg, t0, t1) in enumerate(st):
                            n = t1 - t0
                            hc = seg.hc0 + t0 - 1
                            bg = bank()
                            bv = bank()
                            for k in range(KC):
                                P.mm(bg[:, 0:n + 2], slab[:, k, jj * 128:(jj + 1) * 128], hT[:, k, hc:hc + n + 2],
                                     start=(k == 0), stop=(k == KC - 1))
                            for k in range(KC):
                                P.mm(bv[:, 0:n + 2], slab[:, k, wv + jj * 128:wv + (jj + 1) * 128], hT[:, k, hc:hc + n + 2],
                                     start=(k == 0), stop=(k == KC - 1))
                            fset = 3 * (ffn_it[0] % 2)
                            ffn_it[0] += 1
                            tg = tmpf[fset + 0]
                            tv = tmpf[fset + 1]
                            sg = tmpf[fset + 2]
                            P.act(tg[:, 0:n], bg[:, 1:n + 1], AF.Identity, scale=fcw[:, l, j, 1:2])
                            P.act(tv[:, 0:n], bv[:, 1:n + 1], AF.Identity, scale=fcw[:, l, NFC + j, 1:2])
                            P.stt("dve", tg[:, 0:n], bg[:, 0:n], fcw[:, l, j, 0:1], tg[:, 0:n], ALU.mult, ALU.add)
                            P.stt("dve", tg[:, 0:n], bg[:, 2:n + 2], fcw[:, l, j, 2:3], tg[:, 0:n], ALU.mult, ALU.add)
                            P.stt("dve", tv[:, 0:n], bv[:, 0:n], fcw[:, l, NFC + j, 0:1], tv[:, 0:n], ALU.mult, ALU.add)
                            P.stt("dve", tv[:, 0:n], bv[:, 2:n + 2], fcw[:, l, NFC + j, 2:3], tv[:, 0:n], ALU.mult, ALU.add)
                            if pending is not None:
                                stage_b(pending)
                            pending = (tg, tv, sg, n, j, offs[ti])
                stage_b(pending)
                for s in range(4):
                    wb = wdn_bufs[s % 2]
                    if s >= 2:
                        P.dma("pool", wb.whole(), dram(dn_ap[:, :, 256 * s:256 * (s + 1)]))
                    for oo in range(2):
                        o = 2 * s + oo
                        for ti, (seg, t0, t1) in enumerate(st):
                            n = t1 - t0
                            xc = seg.xc0 + t0
                            b = bank()
                            for k in range(NFC):
                                P.mm(b[:, 0:n], wb[:, k, oo * 128:(oo + 1) * 128], actT[:, k, offs[ti]:offs[ti] + n],
                                     start=(k == 0), stop=(k == NFC - 1))
                            P.stt("dve", xT[:, o, xc:xc + n], b[:, 0:n], mod_ap(l, 5, o, seg.cond),
                                  xT[:, o, xc:xc + n], ALU.mult, ALU.add)

        with phase() as ph:
            xst = [sb(f"xst{i}", [128, D], stack=ph) for i in range(2)]
            n0_sq = [sb(f"sqb0{i}", [128, KC, 384], BF16, stack=ph) for i in range(2)]
            n0_rstd = sb("rstdall0m", [128, XC], stack=ph)
            n0_tmp = [sb(f"tmpn0m_{i}", [128, 512], stack=ph) for i in range(2)]
            n0_rng = [(SEG_S, NS_IN), (SEG_P0, NPR), (SEG_P1, NPR)]
            xblocks = []
            for seg, dsrc in ((SEG_S, d_xs), (SEG_P0, d_xp), (SEG_P1, d_xp)):
                roff = NPR if seg is SEG_P1 else 0
                for (t0, t1) in split_blocks(seg.n):
                    xblocks.append((seg, dsrc, roff, t0, t1))

            def load_block(bi):
                seg, dsrc, roff, t0, t1 = xblocks[bi]
                nt = t1 - t0
                st = xst[bi % 2]
                P.dma("sp", st[0:nt, :], dram(dsrc.ap()[roff + t0:roff + t1, :]))
                for half in range(2):
                    b = bank()
                    for kk in range(4):
                        k = half * 4 + kk
                        P.transpose(b[:, kk * 128:kk * 128 + nt], st[0:nt, k * 128:(k + 1) * 128], ident[0:nt, 0:nt])
                    src = reg(b[:, :], b.h[:, :].rearrange("p (a c) -> p a c", a=4)[:, :, 0:nt])
                    dst = xT[:, half * 4:half * 4 + 4, seg.xc0 + t0:seg.xc0 + t1]
                    P.copy("act" if half == 0 else "dve", dst, src)

            w_issue(("ada", 0, 0), src_3d(d_adaw, 0), [(0, D)])
            w_issue(("ada", 0, 1), src_3d(d_adaw, 0), [(D, 2 * D)])
            nb_ = len(xblocks)
            done_ = 0
            for m in range(6):
                upto = min(nb_, (nb_ * (m + 1) + 3) // 4)
                while done_ < upto:
                    load_block(done_)
                    done_ += 1
                slab = w_get(("ada", 0, m), src_3d(d_adaw, 0), [(m * D, (m + 1) * D)])
                ada_slab(0, m, slab)
                if m + 2 < 6:
                    w_issue(("ada", 0, m + 2), src_3d(d_adaw, 0), [((m + 2) * D, (m + 3) * D)])
                elif m == 4:
                    w_issue(("win_e", 0), src_2d(d_win_e), [(0, 768)])
                if m == 3:
                    assert done_ == nb_
                    norm_mod_batched(0, 0, n0_rng, n0_sq, n0_rstd, None, part="stats")
            assert done_ == nb_
            ada_gmod(0)
            norm_mod_batched(0, 0, n0_rng, n0_sq, n0_rstd, tmpa_main + n0_tmp, part="apply")


        def l0_body():
          if stop_after != "load":
            with phase() as ph:
                mixT = sb("mixT", [128, KC, XC], BF16, stack=ph)
                KTd = sb("KTd", [128, 2, XC], BF16, stack=ph)
                Vtok = sb("Vtok", [128, NVB, 2, 128], BF16, stack=ph)
                KcTd = sb("KcTd", [128, 2, 512], BF16, stack=ph)
                Vc = sb("Vc", [128, 4, 2, 128], BF16, stack=ph)
                PT = [sb(f"PT{i}", [128, 512], BF16, stack=ph) for i in range(3)]
                masks = sb("masks", [128, 2, 512], BF16, stack=ph)
                cosT = sb("cosT", [128, NS_IN], BF16, stack=ph)
                sinT = sb("sinT", [128, NS_IN], BF16, stack=ph)
                sink4 = sb("sink4", [4, 2, 64], stack=ph)
                zo4 = sb("zo4", [4, 2, 128], BF16, stack=ph)
                sind = sb("sind", [4, 512], BF16, stack=ph)
                knf = sb("knf", [128, 2 * NPR], stack=ph)
                tq = [sb(f"tq{i}", [128, 512], stack=ph) for i in range(4)]
                qnb = sb("qnb", [128, 512], BF16, stack=ph)
                qnb2 = sb("qnb2", [128, 512], BF16, stack=ph)
                qb16 = sb("qb16", [128, 512], BF16, stack=ph)
                qb16b = sb("qb16b", [128, 512], BF16, stack=ph)
                tq2b = sb("tq2b", [128, 512], stack=ph)
                tq3b = sb("tq3b", [128, 512], stack=ph)
                rope_it = [0]
                rstdq = sb("rstdq", [128, 512], stack=ph)
                tq0b = sb("tq0b", [128, 512], stack=ph)
                qset = [0]

                P.dma("pool", cosT.whole(), dram(d_cos.ap()))
                P.dma("pool", sinT.whole(), dram(d_sin.ap()))
                P.dma("sp", sink4.whole(), dram(d_sink.ap()))
                P.dma("pool", sind.whole(), dram(d_sind.ap()))
                P.dma("pool", masks.whole(), dram(d_msk.ap()))
                P.memset("dve", zo4.whole(), 0.0)
                P.act(zo4[:, :, 64:128], sink4.whole(), AF.Exp)
                P.memset("pool", Vtok.whole(), 1.0)
                P.memset("pool", Vc.whole(), 1.0)

                if True:
                    cst = tq[0]
                    cdup = tq[1]
                    for h_ in range(2):
                        P.dma("pool", Vc[:, :, h_, 0:64],
                              dram(d_cv.ap().rearrange("(b p) c -> p b c", p=128)[:, :, h_ * 64:(h_ + 1) * 64]))
                    P.dma("sp", reg(cst[:, :], cst.h[:, :].rearrange("p (b c) -> p b c", b=4)),
                          dram(d_ck.ap().rearrange("(b p) c -> p b c", p=128)))
                    for cb in range(4):
                        for h in range(2):
                            P.copy("dve", cdup[:, h * 128:h * 128 + 64], cst[:, cb * 128 + h * 64:cb * 128 + (h + 1) * 64])
                            P.copy("act", cdup[:, h * 128 + 64:h * 128 + 128], cst[:, cb * 128 + h * 64:cb * 128 + (h + 1) * 64])
                        b = bank()
                        for h in range(2):
                            P.transpose(b[:, h * 128:(h + 1) * 128], cdup[:, h * 128:(h + 1) * 128], ident[:, :])
                        for h in range(2):
                            P.copy("dve" if h == 0 else "act", KcTd[:, h, cb * 128:(cb + 1) * 128], b[:, h * 128:(h + 1) * 128])


                if stop_after == "l0a":
                    raise _Stop()
                src_e = src_2d(d_win_e)
                slabA = w_get(("win_e", 0), src_e, [(0, 768)])
                w_issue(("win_e", 1), src_e, [(768, 1024), (1280, 1536), (1792, 2048)])
                in_tiles = seg_tiles([(SEG_S, NS_IN), (SEG_P0, NPR), (SEG_P1, NPR)], 510)

                def qk_norm_a(bz, n, si_):
                    sq_ = qnb if si_ == 0 else qnb2
                    P.act(sq_[:, 0:n], bz[:, 0:n], AF.Square)
                    b2 = bank()
                    P.mm(b2[:, 0:n], bones_bf[:, :], sq_[:, 0:n])
                    return b2

                def qk_norm_b(bz, b2, n, gcol, dst_f, si_):
                    rs_ = rstd if si_ == 0 else rstdq
                    P.act(rs_[:, 0:n], b2[:, 0:n], AF.Ln, bias=epst[:, 0:1], scale=1.0 / 64)
                    P.act(rs_[:, 0:n], rs_[:, 0:n], AF.Exp, scale=-0.5)
                    P.stt("dve", dst_f[:, 0:n], bz[:, 0:n], qkg[:, gcol:gcol + 1], rs_[:, 0:n], ALU.mult, ALU.mult)

                def rope(src_f, n, t0, dst):
                    qb_, t2_, t3_ = (qb16, tq[2], tq[3]) if rope_it[0] % 2 == 0 else (qb16b, tq2b, tq3b)
                    rope_it[0] += 1
                    P.copy("act", qb_[:, 0:n], src_f[:, 0:n])
                    b3 = bank()
                    P.mm(b3[:, 0:n], rmat_bf[:, :], qb_[:, 0:n])
                    P.tt("pool", t2_[:, 0:n], src_f[:, 0:n], cosT[:, t0:t0 + n], ALU.mult)
                    P.tt("dve", t3_[:, 0:n], b3[:, 0:n], sinT[:, t0:t0 + n], ALU.mult)
                    P.tt("pool", dst, t2_[:, 0:n], t3_[:, 0:n], ALU.add)

                for (seg, t0, t1) in in_tiles:
                    n = t1 - t0
                    xc = seg.xc0 + t0
                    hc = seg.hc0 + t0
                    is_s = seg is SEG_S
                    for grp in ((0, 1), (2, 3), (4,)):
                        zb = {}
                        for c in grp:
                            bz = bank()
                            for k in range(KC):
                                P.mm(bz[:, 0:n], slabA[:, k, c * 128:(c + 1) * 128], hT[:, k, hc:hc + n],
                                     start=(k == 0), stop=(k == KC - 1))
                            zb[c] = bz
                        sb2 = {}
                        for si_, c in enumerate(grp):
                            sb2[c] = qk_norm_a(zb[c], n, si_)
                        fdst = {}
                        for si_, c in enumerate(grp):
                            if c == 4 and not is_s:
                                fdst[c] = tq[1]
                            else:
                                fdst[c] = tq[0] if si_ == 0 else tq0b
                            qk_norm_b(zb[c], sb2[c], n, 0 if c < 4 else 1, fdst[c], si_)
                        for si_, c in enumerate(grp):
                            f0 = fdst[c]
                            if c < 4:
                                if is_s:
                                    rope(f0, n, t0, mixT[:, c, xc:xc + n])
                                else:
                                    P.copy("act", mixT[:, c, xc:xc + n], f0[:, 0:n])
                            else:
                                if is_s:
                                    rope(f0, n, t0, tq[1][:, 0:n])
                                else:
                                    pc = xc - NS_IN
                                    P.copy("act", knf[:, pc:pc + n], tq[1][:, 0:n])
                                kf = tq[1]
                                P.copy("act", KTd[0:64, 0, xc:xc + n], kf[0:64, 0:n])
                                P.copy("dve", KTd[64:128, 0, xc:xc + n], kf[0:64, 0:n])
                                P.copy("act", KTd[0:64, 1, xc:xc + n], kf[64:128, 0:n])
                                P.copy("dve", KTd[64:128, 1, xc:xc + n], kf[64:128, 0:n])
                for seg in SEGS:
                    for bi, (t0, t1) in enumerate(split_blocks(seg.n)):
                        nt = t1 - t0
                        hc = seg.hc0 + t0
                        bz = bank()
                        for k in range(KC):
                            P.mm(bz[0:nt, 0:128], hT[:, k, hc:hc + nt], slabA[:, k, 640:768],
                                 start=(k == 0), stop=(k == KC - 1))
                        P.copy("act", Vtok[0:nt, seg.vb0 + bi, :, 0:64],
                               reg(bz[0:nt, 0:128], bz.h[0:nt, 0:128].rearrange("p (h d) -> p h d", h=2)))
                        if seg is not SEG_S:
                            pb = seg.vb0 - 11 + bi
                            P.copy("dve", tq[3][0:nt, pb * 128:(pb + 1) * 128], bz[0:nt, 0:128])
                P.dma("sp", dram(d_nv.ap().rearrange("(b p) c -> p b c", p=128)),
                      reg(tq[3][:, :], tq[3].h[:, :].rearrange("p (b c) -> p b c", b=4)), final=True)
                if True:
                    kst = tq[2]
                    for pb in range(4):
                        b = bank()
                        P.transpose(b[:, 0:128], knf[:, pb * 128:(pb + 1) * 128], ident[:, :])
                        P.copy("dve", kst[:, pb * 128:(pb + 1) * 128], b[:, 0:128])
                    P.dma("sp", dram(d_nk.ap().rearrange("(b p) c -> p b c", p=128)),
                          reg(kst[:, :], kst.h[:, :].rearrange("p (b c) -> p b c", b=4)), final=True)

                if stop_after == "l0b":
                    raise _Stop()
                for cs in range(2):
                    slab = w_get(("win_e", 1 + cs), src_e, None)
                    if cs == 0:
                        w_issue(("win_e", 2), src_e, [(1024, 1280), (1536, 1792), (2048, 2304)])
                    else:
                        w_issue(("ada", 1, 0), src_3d(d_adaw, 1), [(0, D)])
                    for jj in range(2):
                        j = cs * 2 + jj
                        for (seg, t0, t1) in in_tiles:
                            n = t1 - t0
                            xc = seg.xc0 + t0
                            hc = seg.hc0 + t0 - 1
                            bB = bank()
                            bC = bank()
                            bH = bank()
                            for bi, bb in enumerate((bB, bC, bH)):
                                co = bi * 256 + jj * 128
                                for k in range(KC):
                                    P.mm(bb[:, 0:n + 2], slab[:, k, co:co + 128], hT[:, k, hc:hc + n + 2],
                                         start=(k == 0), stop=(k == KC - 1))
                            P.copy("act", tq[0][:, 0:n + 2], bH[:, 0:n + 2])
                            P.tt("dve", tq[1][:, 0:n + 2], bC[:, 0:n + 2], tq[0][:, 0:n + 2], ALU.mult)
                            P.act(tq[2][:, 0:n], tq[1][:, 1:n + 1], AF.Identity, scale=scw[:, j, 1:2])
                            P.stt("dve", tq[2][:, 0:n], tq[1][:, 0:n], scw[:, j, 0:1], tq[2][:, 0:n], ALU.mult, ALU.add)
                            P.stt("dve", tq[2][:, 0:n], tq[1][:, 2:n + 2], scw[:, j, 2:3], tq[2][:, 0:n], ALU.mult, ALU.add)
                            P.tt("dve", mixT[:, 4 + j, xc:xc + n], bB[:, 1:n + 1], tq[2][:, 0:n], ALU.mult)

                if stop_after == "l0d":
                    raise _Stop()
                units = []
                it = 0
                for seg in SEGS:
                    is_s = seg is SEG_S
                    nq = NS_QA if is_s else NPR
                    nk = NS_IN if is_s else NPR
                    for qb in range(nq // 128):
                        for h in range(2):
                            kbs = []
                            if is_s:
                                for kb, mk in ((qb - 1, 0), (qb, None), (qb + 1, 1)):
                                    if kb < 0 or kb * 128 >= nk:
                                        continue
                                    kbs.append(("loc", kb, min(128, nk - kb * 128), mk))
                                for cb in range(4):
                                    kbs.append(("ctx", cb, 128, None))
                            else:
                                kbs = [("loc", 0, 128, None), ("loc", 1, 128, None)]
                            for ki, (kind, kb, kn, mk) in enumerate(kbs):
                                units.append(dict(seg=seg, q0=seg.xc0 + qb * 128, h=h, ki=ki, nk=len(kbs), kind=kind,
                                                  kb=kb, kn=kn, mk=mk, it=it, idx=len(units)))
                            it += 1

                def emit_S(u):
                    u["Sbe"] = banks[2 * (u["idx"] % 3)]
                    u["Sbo"] = banks[2 * (u["idx"] % 3) + 1]
                    seg, h, kb, kn, q0 = u["seg"], u["h"], u["kb"], u["kn"], u["q0"]
                    mk = u["mk"]
                    for half in range(2):
                        if u["kind"] == "loc":
                            kt = KTd[half * 64:(half + 1) * 64, h, seg.xc0 + kb * 128:seg.xc0 + kb * 128 + kn]
                        else:
                            kt = KcTd[half * 64:(half + 1) * 64, h, kb * 128:kb * 128 + kn]
                        Sb = u["Sbe"] if half == 0 else u["Sbo"]
                        qv = mixT[half * 64:(half + 1) * 64, 2 * h:2 * h + 2, q0:q0 + 128]
                        so = reg(Sb[0:kn, 0:256], Sb.h[0:kn, 0:256].rearrange("p (a c) -> p a c", a=2))
                        P.mm(so, kt, qv)

                def emit_rest(u):
                    seg, h, kb, kn, q0, ki, it_ = u["seg"], u["h"], u["kb"], u["kn"], u["q0"], u["ki"], u["it"]
                    pt = PT[u["idx"] % 3]
                    P.act(pt[0:kn, 0:256], u["Sbe"][0:kn, 0:256], AF.Exp)
                    P.act(pt[0:kn, 256:512], u["Sbo"][0:kn, 0:256], AF.Exp)
                    if u["mk"] is not None:
                        P.tt("pool", pt[0:kn, :], pt[0:kn, :], masks[0:kn, u["mk"], :], ALU.mult)
                    OD = banks[6 + it_ % 2]
                    if u["kind"] == "loc":
                        vt = Vtok[0:kn, seg.vb0 + kb, h, :]
                    else:
                        vt = Vc[0:kn, kb, h, :]
                    if ki == 0:
                        P.mm(OD[:, :], zo4[0:4, h, :], sind[0:4, :], start=True, stop=False)
                    P.mm(OD[:, :], vt, pt[0:kn, :], start=False, stop=(ki == u["nk"] - 1))
                    if ki == u["nk"] - 1:
                        den = tq[it_ % 2]
                        on = tq[2 + it_ % 2]
                        if seg is SEG_S:
                            P.recip(den[64:128, :], OD[64:128, :])
                        else:
                            P.act(on[64:128, :], OD[64:128, :], AF.Ln)
                            P.act(den[64:128, :], on[64:128, :], AF.Exp, scale=-1.0)
                        P.copy("dve", den[0:64, :], den[64:128, :])
                        for slot, g in enumerate(SLOT_G):
                            half = g % 2
                            ch = 2 * h + g // 2
                            if half == 0:
                                P.tt("dve", mixT[0:64, ch, q0:q0 + 128], OD[0:64, slot * 128:(slot + 1) * 128],
                                     den[0:64, slot * 128:(slot + 1) * 128], ALU.mult)
                        P.tt("dve", on[0:64, 256:512], OD[0:64, 256:512], den[0:64, 256:512], ALU.mult)
                        for slot, g in enumerate(SLOT_G):
                            half = g % 2
                            ch = 2 * h + g // 2
                            if half == 1:
                                P.copy("dve", mixT[64:128, ch, q0:q0 + 128], on[0:64, slot * 128:(slot + 1) * 128])

                LOOK = 2
                ada_every = max(1, len(units) // 7)
                ada_m = 0
                for i in range(len(units) + LOOK):
                    if i < len(units):
                        emit_S(units[i])
                    if i - LOOK >= 0:
                        emit_rest(units[i - LOOK])
                    if i > 0 and i % ada_every == 0 and ada_m < 6:
                        slab = w_get(("ada", 1, ada_m), src_3d(d_adaw, 1), [(ada_m * D, (ada_m + 1) * D)])
                        if ada_m < 5:
                            w_issue(("ada", 1, ada_m + 1), src_3d(d_adaw, 1), [((ada_m + 1) * D, (ada_m + 2) * D)])
                        else:
                            w_issue(("wout_e", 0), src_2d(d_wout_e), [(0, D)])
                        ada_slab(1, ada_m, slab, mb=banks[2 * ((i + 1) % 3)])
                        ada_m += 1
                assert ada_m == 6
                ada_gmod(1)

                if stop_after == "l0e":
                    raise _Stop()
                slab = w_get(("wout_e", 0), None, None)
                w_issue(("wup", 0, 0, 0), src_3d(d_wup, 0), [(0, 512), (DFF, DFF + 512)])
                ktd_flat = KTd.h[:, :, :].rearrange("p a c -> p (a c)")[:, 0:KC * 384].rearrange("p (k c) -> p k c", k=KC)
                vtk_flat = Vtok.h[:, :, :, :].rearrange("p a b c -> p (a b c)")[:, 0:KC * 384].rearrange("p (k c) -> p k c", k=KC)
                sq_views = [(KTd.whole(), ktd_flat), (Vtok.whole(), vtk_flat)]
                rs_bufs = [tq[0], tq[1], tq0b, tq2b, tq3b, rstdq]
                tm_bufs = tmpa_main + [tq[2], tq[3]]
                otl = seg_tiles([(SEG_S, NS_X1), (SEG_P0, NPR), (SEG_P1, NPR)], 384)
                assert len(otl) <= len(rs_bufs)
                def op_tile(ti_):
                    seg, t0, t1 = otl[ti_]
                    n = t1 - t0
                    xc = seg.xc0 + t0
                    j = seg.cond
                    for o in range(KC):
                        b = bank()
                        for k in range(KC):
                            P.mm(b[:, 0:n], slab[:, k, o * 128:(o + 1) * 128], mixT[:, k, xc:xc + n],
                                 start=(k == 0), stop=(k == KC - 1))
                        P.stt("dve", xT[:, o, xc:xc + n], b[:, 0:n], mod_ap(0, 2, o, j),
                              xT[:, o, xc:xc + n], ALU.mult, ALU.add)

                def nrm_tile(ti_):
                    seg, t0, t1 = otl[ti_]
                    n = t1 - t0
                    xc = seg.xc0 + t0
                    hc = seg.hc0 + t0
                    j = seg.cond
                    sqw, sqa = sq_views[ti_ % 2]
                    P.act(reg(sqw, sqa[:, :, 0:n]), xT[:, :, xc:xc + n], AF.Square)
                    b = bank()
                    for k in range(KC):
                        P.mm(b[:, 0:n], ones_bf[:, :], reg(sqw, sqa[:, k, 0:n]), start=(k == 0), stop=(k == KC - 1))
                    rs = rs_bufs[ti_]
                    P.act(rs[:, 0:n], b[:, 0:n], AF.Ln, bias=epst[:, 0:1], scale=1.0 / D)
                    P.act(rs[:, 0:n], rs[:, 0:n], AF.Exp, scale=-0.5)
                    for k in range(KC):
                        tm = tm_bufs[(k % 2) * 2 + (k // 2) % 2]
                        if k % 2 == 0:
                            P.tt("dve", tm[:, 0:n], xT[:, k, xc:xc + n], rs[:, 0:n], ALU.mult)
                            P.act(hT[:, k, hc:hc + n], tm[:, 0:n], AF.Identity,
                                  bias=mod_ap(0, 3, k, j), scale=gmod[:, 0, 1, k:k + 1, j])
                        else:
                            P.tt("pool", tm[:, 0:n], xT[:, k, xc:xc + n], rs[:, 0:n], ALU.mult)
                            P.ts("dve", hT[:, k, hc:hc + n], tm[:, 0:n], gmod[:, 0, 1, k:k + 1, j],
                                 mod_ap(0, 3, k, j), ALU.mult, ALU.add)

                op_tile(0)
                for ti_ in range(len(otl)):
                    if ti_ + 1 < len(otl):
                        op_tile(ti_ + 1)
                    nrm_tile(ti_)

        try:
            l0_body()
        except _Stop:
            pass

        def ffn_phase(l, ranges_norm, ranges_out, next_key_issue):
            if ranges_norm is not None:
              with phase() as ph:
                  sqb = sb(f"sqbf{l}", [128, KC, 384], BF16, stack=ph)
                  sqb2 = sb(f"sqbg{l}", [128, KC, 384], BF16, stack=ph)
                  rstd_all = sb(f"rstdall{l}", [128, XC], stack=ph)
                  tmp2 = [sb(f"tmpn{l}_{i}", [128, 512], stack=ph) for i in range(2)]
                  norm_mod_batched(l, 1, ranges_norm, [sqb, sqb2], rstd_all, tmpa_main + tmp2)
            with phase() as ph:
                actT = sb(f"actT{l}", [128, NFC, 900], BF16, stack=ph)
                wdn_bufs = [sb(f"wdn{l}_{i}", [128, NFC, 256], BF16, stack=ph) for i in range(2)]
                tmpf = [sb(f"tmpf{l}_{i}", [128, 512], stack=ph) for i in range(6)]
                ffn(l, ranges_out, wdn_bufs, actT, tmpf)
                if next_key_issue is not None:
                    next_key_issue()

        EARLY = ("load", "l0a", "l0b", "l0d", "l0e", "l0mix")
        if stop_after not in EARLY:
            ffn_phase(0, None,
                      [(SEG_S, NS_F0), (SEG_P0, NPR), (SEG_P1, NPR)],
                      lambda: w_issue(("win_o", 0), src_2d(d_win_o), [(0, D)]))

        rng1 = [(SEG_S, NS_G1), (SEG_P0, NPR), (SEG_P1, NPR)]
        if stop_after not in EARLY + ("l0",):
            with phase() as phn:
                sqb = sb("sqb1", [128, KC, 384], BF16, stack=phn)
                sqb2 = sb("sqb1b", [128, KC, 384], BF16, stack=phn)
                rstd_all = sb("rstdall1m", [128, XC], stack=phn)
                tmp2 = [sb(f"tmpn1b_{i}", [128, 512], stack=phn) for i in range(2)]
                norm_mod_batched(1, 0, rng1, [sqb, sqb2], rstd_all, tmpa_main + tmp2)
            with phase() as ph:
                uT = sb("uT", [128, KC, XC], BF16, stack=ph)
                gvn = sb("gvn", [128, D], stack=ph)
                bsp = sb("bsp", [128, 8, 128], stack=ph)
                wst = sb("wst", [128, 8, 128], BF16, stack=ph)
                gv_s = [sb(f"gv{i}", [128, D], stack=ph) for i in range(2)]
                junk_s = [sb(f"junk{i}", [128, D], BF16, stack=ph) for i in range(2)]
                vn_s = [sb(f"vn{i}", [128, D], BF16, stack=ph) for i in range(2)]
                ssq_s = [sb(f"ssq{i}", [128, 2], stack=ph) for i in range(2)]
                tg = [sb(f"tg{i}", [128, 512], stack=ph) for i in range(4)]
                gchunk = [0]
                P.dma("sp", gvn.whole(), dram(d_gvn.ap()))
                P.dma("sp", bsp.whole(), dram(d_bsp.ap()))
                P.dma("pool", wst.whole(), dram(d_wst.ap()))
                src_o = src_2d(d_win_o)
                slabU = w_get(("win_o", 0), src_o, [(0, D)])
                slabV = w_issue(("win_o", 1), src_o, [(D, 2 * D)])
                ws_state["pref"].pop(("win_o", 1))
                for o in range(KC):
                    for (seg, t0, t1) in seg_tiles(rng1, 512):
                        n = t1 - t0
                        b = bank()
                        for k in range(KC):
                            P.mm(b[:, 0:n], slabU[:, k, o * 128:(o + 1) * 128], hT[:, k, seg.hc0 + t0:seg.hc0 + t1],
                                 start=(k == 0), stop=(k == KC - 1))
                        P.act(uT[:, o, seg.xc0 + t0:seg.xc0 + t1], b[:, 0:n], AF.Gelu_apprx_tanh)
                def g_stage_a(seg, cb, cp):
                    t0 = cb * 128
                    hc = seg.hc0 + t0
                    gv, junk, vn, ssq = gv_s[cp], junk_s[cp], vn_s[cp], ssq_s[cp]
                    for half in range(2):
                        b = bank()
                        for k in range(KC):
                            P.mm(b[:, :], hT[:, k, hc:hc + 128], slabV[:, k, half * 512:(half + 1) * 512],
                                 start=(k == 0), stop=(k == KC - 1))
                        P.act(gv[:, half * 512:(half + 1) * 512], b[:, :], AF.Gelu_apprx_tanh)
                    P.memset("dve", ssq[:, 0:1], 0.0)
                    P.act(junk[:, :], gv[:, :], AF.Square, accum_out=ssq[:, 0:1])
                    P.act(ssq[:, 1:2], ssq[:, 0:1], AF.Sqrt, bias=epst[:, 0:1], scale=1.0 / D)
                    P.recip(ssq[:, 1:2], ssq[:, 1:2])
                    P.stt("dve", vn[:, :], gv[:, :], ssq[:, 1:2], gvn[:, :], ALU.mult, ALU.mult)

                def g_stage_b(seg, cb, cp):
                    xc = seg.xc0 + cb * 128
                    vn = vn_s[cp]
                    for gh in range(2):
                        b = bank()
                        for gg in range(4):
                            g = gh * 4 + gg
                            P.mm(b[:, gg * 128:(gg + 1) * 128], vn[:, g * 128:(g + 1) * 128], wst[:, g, :])
                        tt_ = tg[2 * cp + gh]
                        bview = reg(bsp[:, gh * 4:gh * 4 + 4, :], bsp.h[:, gh * 4:gh * 4 + 4, :].rearrange("p a c -> p (a c)"))
                        P.tt("dve", tt_[:, :], b[:, :], bview, ALU.add)
                        tview = reg(tt_[:, :], tt_.h[:, :].rearrange("p (a c) -> p a c", a=4))
                        uv = uT[:, gh * 4:gh * 4 + 4, xc:xc + 128]
                        P.tt("pool", uv, uv, tview, ALU.mult)

                w_issue(("wout_o", 0), src_2d(d_wout_o), [(0, D)])
                chunks = [(seg, cb) for seg, ntok in rng1 for cb in range(ntok // 128)]
                for i, (seg, cb) in enumerate(chunks):
                    g_stage_a(seg, cb, i % 2)
                    if i > 0:
                        g_stage_b(chunks[i - 1][0], chunks[i - 1][1], (i - 1) % 2)
                g_stage_b(chunks[-1][0], chunks[-1][1], (len(chunks) - 1) % 2)
                slabO = w_get(("wout_o", 0), src_2d(d_wout_o), [(0, D)])
                w_issue(("wup", 1, 0, 0), src_3d(d_wup, 1), [(0, 512), (DFF, DFF + 512)])
                out_proj_residual(slabO, uT, rng1, 1, 2)

        if stop_after not in EARLY + ("l0", "l1mix"):
            ffn_phase(1, [(SEG_S, NS_F1 + 1), (SEG_P0, NPR), (SEG_P1, NPR)],
                      [(SEG_S, NS_F1), (SEG_P0, NPR), (SEG_P1, NPR)], None)

        with phase() as ph:
            yst = [sb(f"yst{i}", [128, D], stack=ph) for i in range(2)]
            blk = 0
            for seg, ddst, nout in ((SEG_S, d_ys, NS_F1), (SEG_P0, d_yp, NPR), (SEG_P1, d_yp, NPR)):
                roff = NPR if seg is SEG_P1 else 0
                for tb in range(nout // 128):
                    t0 = tb * 128
                    st = yst[blk % 2]
                    blk += 1
                    for half in range(2):
                        b = bank()
                        for kk in range(4):
                            k = half * 4 + kk
                            P.transpose(b[:, kk * 128:(kk + 1) * 128], xT[:, k, seg.xc0 + t0:seg.xc0 + t0 + 128], ident[:, :])
                        P.copy("act" if half == 0 else "dve", st[:, half * 512:(half + 1) * 512], b[:, :])
                    P.dma("sp", dram(ddst.ap()[roff + t0:roff + t0 + 128, :]), st.whole(), final=True)

        P.lower(sems)
        run_prog(nc, P)
    return nc, P


def split_blocks(n):
    out = []
    t = 0
    while t < n:
        out.append((t, min(t + 128, n)))
        t += 128
    return out


_CACHE = {}


def _consts():
    ident = np.eye(128, dtype=np.float32)
    ones = np.ones((128, 128), np.float32)
    bones = np.zeros((128, 128), np.float32)
    bones[:64, :64] = 1.0
    bones[64:, 64:] = 1.0
    rmat = np.zeros((128, 128), np.float32)
    for m in range(128):
        j = m % 32
        if j < 16:
            rmat[m + 16, m] = -1.0
        else:
            rmat[m - 16, m] = 1.0
    cst = np.stack([ident, ones, bones, rmat], axis=1)
    kk = np.arange(128)[:, None]
    qq = np.arange(128)[None, :]
    m_prev = (qq <= kk).astype(np.float32)
    m_next = (kk <= qq).astype(np.float32)
    masks = np.stack([np.tile(m_prev, (1, 4)), np.tile(m_next, (1, 4))], axis=1)
    return np.ascontiguousarray(cst), np.ascontiguousarray(masks)


def _rope_tables(pos):
    n_freq = 16
    inv = 10000.0 ** (-np.arange(n_freq, dtype=np.float64) / n_freq)
    row = (pos // 64).astype(np.float64)
    col = (pos % 64).astype(np.float64)
    ra = row[None, :] * inv[:, None]
    ca = col[None, :] * inv[:, None]
    ang = np.zeros((64, len(pos)), np.float64)
    for d in range(64):
        f = d % 16
        ang[d] = ra[f] if d < 32 else ca[f]
    ang = np.concatenate([ang, ang], axis=0)
    return np.cos(ang).astype(np.float32), np.sin(ang).astype(np.float32)


def kernel(x_prompt, x_sample, cache_k, cache_v, c, c_ctx, ada_w, ada_b, norm_mix_g, norm_ffn_g,
           w_in_even, q_norm_g, k_norm_g, sink_logit, short_conv_w, w_out_even,
           w_in_odd, gmlp_norm_g, w_spatial, b_spatial, w_out_odd, w_up, ffn_conv_w, w_down, _stop_after=None):
    f = lambda a: np.ascontiguousarray(np.asarray(a, dtype=np.float32))
    x_prompt, x_sample, cache_k, cache_v = f(x_prompt), f(x_sample), f(cache_k), f(cache_v)
    c, c_ctx, ada_w, ada_b = f(c), f(c_ctx), f(ada_w), f(ada_b)
    key = ("nc", _stop_after)
    if key not in _CACHE:
        _CACHE[key] = build(_stop_after)
    nc, _ = _CACHE[key]
    cst, masks = _consts()

    def fm(v, nchunk):
        return np.ascontiguousarray(f(v).reshape(nchunk, 128).T)

    shared = {
        "ada_w": ada_w,
        "ada_bT": np.ascontiguousarray(np.stack([fm(ada_b[l], 48) for l in range(2)], axis=1)),
        "gmixT": np.ascontiguousarray(np.stack([fm(norm_mix_g[l], 8) for l in range(2)], axis=1)),
        "gffnT": np.ascontiguousarray(np.stack([fm(norm_ffn_g[l], 8) for l in range(2)], axis=1)),
        "w_in_even": f(w_in_even)[0], "w_out_even": f(w_out_even)[0],
        "w_in_odd": f(w_in_odd)[0], "w_out_odd": f(w_out_odd)[0],
        "w_up": f(w_up), "w_down": f(w_down),
        "qkg": np.ascontiguousarray(np.stack([np.tile(f(q_norm_g)[0], 2), np.tile(f(k_norm_g)[0], 2)], axis=1)),
        "sink4": np.ascontiguousarray(np.broadcast_to(
            f(sink_logit)[0].reshape(2, 4)[:, list(SLOT_G)].T[:, :, None], (4, 2, 64))),
        "slotind": np.ascontiguousarray(np.repeat(np.eye(4, dtype=np.float32), 128, axis=1)),
        "gvn_b": np.ascontiguousarray(np.broadcast_to(f(gmlp_norm_g)[0][None, :], (128, D))),
        "consts": cst, "masks": masks,
    }
    scw_n = f(short_conv_w)[0]
    fcw_n = f(ffn_conv_w)
    wsp = f(w_spatial)[0]
    bspn = f(b_spatial)[0]

    def variant(mirror):
        tap = [2, 1, 0] if mirror else [0, 1, 2]
        scw = np.ascontiguousarray(scw_n[tap].reshape(3, 4, 128).transpose(2, 1, 0))
        fcw = np.ascontiguousarray(fcw_n[:, tap].reshape(2, 3, 44, 128).transpose(3, 0, 2, 1))
        w = wsp[:, ::-1, ::-1] if mirror else wsp
        w_sT = np.ascontiguousarray(w.transpose(2, 0, 1))
        b = bspn[:, ::-1] if mirror else bspn
        bsp = np.ascontiguousarray(np.broadcast_to(b[None], (128, 8, 128)))
        return {"scw": scw, "fcw": fcw, "w_sT": w_sT, "bsp": bsp}

    var = [variant(False), variant(True)]
    in_maps = []
    for core in range(8):
        b = core // 2
        mirror = core % 2 == 1
        if not mirror:
            pos = np.arange(0, NS_IN)
        else:
            pos = 2047 - np.arange(0, NS_IN)
        xs = np.ascontiguousarray(x_sample[b][pos])
        cos, sin = _rope_tables(pos)
        cond = np.ascontiguousarray(np.stack([c[b].reshape(8, 128).T, c_ctx.reshape(8, 128).T], axis=2))
        m = dict(shared)
        m.update(var[1 if mirror else 0])
        m.update({
            "xs": xs,
            "xp": np.ascontiguousarray((x_prompt[2 * core:2 * core + 2][:, ::-1] if mirror
                                        else x_prompt[2 * core:2 * core + 2]).reshape(2 * NPR, D)),
            "ck": np.ascontiguousarray(cache_k[b, 0].reshape(512, 128)),
            "cv": np.ascontiguousarray(cache_v[b, 0].reshape(512, 128)),
            "cond": cond, "cos": cos, "sin": sin,
        })
        in_maps.append(m)
    res = run_bass_kernel_spmd(nc, in_maps, core_ids=list(range(8)))
    yp = np.zeros((16, NPR, D), np.float32)
    ys = np.zeros((4, 2048, D), np.float32)
    nk = np.zeros((16, 1, NPR, 2, 64), np.float32)
    nv = np.zeros((16, 1, NPR, 2, 64), np.float32)
    for core in range(8):
        r = res.results[core]
        b = core // 2
        sl = slice(None, None, -1) if core % 2 == 1 else slice(None)
        yp[2 * core:2 * core + 2] = r["yp"].reshape(2, NPR, D)[:, sl]
        nk[2 * core:2 * core + 2, 0] = r["nk"].reshape(2, NPR, 2, 64)[:, sl]
        nv[2 * core:2 * core + 2, 0] = r["nv"].reshape(2, NPR, 2, 64)[:, sl]
        if core % 2 == 0:
            ys[b, 0:1024] = r["ys"]
        else:
            ys[b, 1024:2048] = r["ys"][::-1]
    return (yp, ys, nk, nv)
```
